# Optimizing a Trainium2 kernel written in Bass

```python
import math
import jax, jax.numpy as jnp
from jax import lax
import numpy as np

D_MODEL = 1024
BATCH = 2
SEQ = 8192
DEPTH = 2

N_MIXERS = 2
EPS = 1e-6

A_EXPAND = 2
A_WIDTH = A_EXPAND * D_MODEL
A_CHUNK = 128
A_HEADS = 16
A_HEAD_DIM = A_WIDTH // A_HEADS

B_GROUPS = ((128, 1), (512, 4), (2048, 16))
B_N_GROUPS = len(B_GROUPS)
B_HEADS = 16
B_HEAD_DIM = 64
B_WIDTH = B_HEADS * B_HEAD_DIM
B_QKV = B_N_GROUPS * 3 * B_WIDTH
B_TOTAL_HEADS = B_N_GROUPS * B_HEADS

REL_BUCKETS = 32
REL_EXACT = 8
REL_MAX_DISTANCE = 1024

N_A_LAYERS = (DEPTH + 1) // 2
N_B_LAYERS = DEPTH // 2
NEG_INF = -1e30

kernel_name = "hybrid_gmlp_dilated_attn_encoder"


def rms_norm(x, g):
    xf = x.astype(jnp.float32)
    y = xf * lax.rsqrt(jnp.mean(xf * xf, axis=-1, keepdims=True) + EPS)
    return (y * g.astype(jnp.float32)).astype(x.dtype)


def layer_norm(x, g, b):
    xf = x.astype(jnp.float32)
    mu = jnp.mean(xf, axis=-1, keepdims=True)
    xc = xf - mu
    y = xc * lax.rsqrt(jnp.mean(xc * xc, axis=-1, keepdims=True) + EPS)
    return (y * g.astype(jnp.float32) + b.astype(jnp.float32)).astype(x.dtype)


def t5_bucket(rel):
    half = REL_BUCKETS // 2
    ret = jnp.where(rel > 0, half, 0)
    n = jnp.abs(rel)
    nf = jnp.maximum(n, 1).astype(jnp.float32)
    large = REL_EXACT + (jnp.log(nf / REL_EXACT) / math.log(REL_MAX_DISTANCE / REL_EXACT)
                         * (half - REL_EXACT)).astype(jnp.int32)
    large = jnp.minimum(large, half - 1)
    return ret + jnp.where(n < REL_EXACT, n, large)


def gmlp_mixer(h, w_in, w_s, b_s, vn_g, vn_b, w_out):
    bsz, s, _ = h.shape
    z = h @ w_in
    u, v, g = jnp.split(z, 3, axis=-1)
    u = jax.nn.gelu(u)
    v = layer_norm(jax.nn.gelu(v), vn_g, vn_b)
    nc = s // A_CHUNK
    vc = v.reshape(bsz, nc, A_CHUNK, A_HEADS, A_HEAD_DIM)
    sg = jnp.einsum('hpq,bcqhd->bcphd', w_s, vc) + b_s.T[None, None, :, :, None]
    y = u * sg.reshape(bsz, s, A_WIDTH) * jax.nn.silu(g)
    return y @ w_out


def dilated_window_group(q, k, v, table, dilation, half_w):
    bsz, s, nh, dh = q.shape
    L = s // dilation
    W = half_w
    nb = -(-L // W)
    Lp = nb * W

    def strided(t):
        return t.reshape(bsz, L, dilation, nh, dh).transpose(0, 2, 1, 3, 4)

    qs = jnp.pad(strided(q), ((0, 0), (0, 0), (0, Lp - L), (0, 0), (0, 0)))
    pad_kv = ((0, 0), (0, 0), (W, Lp - L + W), (0, 0), (0, 0))
    kp = jnp.pad(strided(k), pad_kv).reshape(bsz, dilation, nb + 2, W, nh, dh)
    vp = jnp.pad(strided(v), pad_kv).reshape(bsz, dilation, nb + 2, W, nh, dh)

    def band(t):
        return jnp.concatenate([t[:, :, 0:nb], t[:, :, 1:nb + 1], t[:, :, 2:nb + 2]], axis=3)

    kb, vb = band(kp), band(vp)
    qb = qs.reshape(bsz, dilation, nb, W, nh, dh)

    logits = jnp.einsum('bnkqhd,bnkjhd->bnkhqj', qb, kb,
                        preferred_element_type=jnp.float32) * (dh ** -0.5)
    qi = jnp.arange(W, dtype=jnp.int32)[:, None]
    kj = jnp.arange(3 * W, dtype=jnp.int32)[None, :]
    rel = kj - W - qi
    bias = table.astype(jnp.float32)[t5_bucket(rel * dilation)].transpose(2, 0, 1)
    key_pos = jnp.arange(nb, dtype=jnp.int32)[:, None] * W + kj - W
    mask = (jnp.abs(rel) <= W)[None] & ((key_pos >= 0) & (key_pos < L))[:, None, :]
    logits = jnp.where(mask[None, None, :, None], logits + bias, NEG_INF)

    m = jnp.max(logits, axis=-1, keepdims=True)
    p = jnp.exp(logits - m)
    den = jnp.sum(p, axis=-1, keepdims=True)
    o = jnp.einsum('bnkhqj,bnkjhd->bnkqhd', p / den, vb.astype(jnp.float32))
    lse = (m + jnp.log(den))[..., 0]

    o = o.reshape(bsz, dilation, Lp, nh, dh)[:, :, :L].transpose(0, 2, 1, 3, 4).reshape(bsz, s, nh, dh)
    lse = lse.transpose(0, 1, 2, 4, 3).reshape(bsz, dilation, Lp, nh)[:, :, :L]
    lse = lse.transpose(0, 2, 1, 3).reshape(bsz, s, nh)
    return o, lse


def dilated_attention_mixer(h, w_in, w_out, rel_table):
    bsz, s, _ = h.shape
    z = h @ w_in
    qkv = z[..., :B_QKV].reshape(bsz, s, B_N_GROUPS, 3, B_HEADS, B_HEAD_DIM)
    gate = z[..., B_QKV:]
    outs, lses = [], []
    for gi, (window, dil) in enumerate(B_GROUPS):
        o, lse = dilated_window_group(qkv[:, :, gi, 0], qkv[:, :, gi, 1], qkv[:, :, gi, 2],
                                      rel_table[:, gi * B_HEADS:(gi + 1) * B_HEADS],
                                      dil, window // (2 * dil))
        outs.append(o)
        lses.append(lse)
    wts = jax.nn.softmax(jnp.stack(lses), axis=0)
    o = jnp.einsum('gbsh,gbshd->bshd', wts, jnp.stack(outs))
    y = o.reshape(bsz, s, B_WIDTH).astype(h.dtype) * jax.nn.silu(gate)
    return y @ w_out


def setup_inputs(seed: int = 0) -> dict:
    key = jax.random.key(seed)
    ks = jax.random.split(key, 16)
    f32 = jnp.float32
    x = jax.random.normal(ks[0], (BATCH, SEQ, D_MODEL), f32)
    norm_pre = 1.0 + 0.1 * jax.random.normal(ks[1], (DEPTH, D_MODEL), f32)
    norm_post = 1.0 + 0.1 * jax.random.normal(ks[2], (DEPTH, D_MODEL), f32)
    a_w_in = jax.random.normal(ks[3], (N_A_LAYERS, D_MODEL, 3 * A_WIDTH), f32) * D_MODEL ** -0.5
    a_w_s = jax.random.normal(ks[4], (N_A_LAYERS, A_HEADS, A_CHUNK, A_CHUNK), f32) * A_CHUNK ** -0.5
    a_b_s = 1.0 + 0.1 * jax.random.normal(ks[5], (N_A_LAYERS, A_HEADS, A_CHUNK), f32)
    a_vnorm_g = 1.0 + 0.1 * jax.random.normal(ks[6], (N_A_LAYERS, A_WIDTH), f32)
    a_vnorm_b = 0.1 * jax.random.normal(ks[7], (N_A_LAYERS, A_WIDTH), f32)
    a_w_out = jax.random.normal(ks[8], (N_A_LAYERS, A_WIDTH, D_MODEL), f32) * A_WIDTH ** -0.5
    b_w_in = jax.random.normal(ks[9], (N_B_LAYERS, D_MODEL, B_QKV + B_WIDTH), f32) * D_MODEL ** -0.5
    b_w_out = jax.random.normal(ks[10], (N_B_LAYERS, B_WIDTH, D_MODEL), f32) * B_WIDTH ** -0.5
    rel_bias = 0.5 * jax.random.normal(ks[11], (REL_BUCKETS, B_TOTAL_HEADS), f32)
    return {"x": x, "norm_pre": norm_pre, "norm_post": norm_post,
            "a_w_in": a_w_in, "a_w_s": a_w_s, "a_b_s": a_b_s,
            "a_vnorm_g": a_vnorm_g, "a_vnorm_b": a_vnorm_b, "a_w_out": a_w_out,
            "b_w_in": b_w_in, "b_w_out": b_w_out, "rel_bias": rel_bias}


def reference(x, norm_pre, norm_post, a_w_in, a_w_s, a_b_s, a_vnorm_g, a_vnorm_b, a_w_out,
              b_w_in, b_w_out, rel_bias):
    for i in range(DEPTH):
        h = rms_norm(x, norm_pre[i])
        j = i // N_MIXERS
        if i % N_MIXERS == 0:
            y = gmlp_mixer(h, a_w_in[j], a_w_s[j], a_b_s[j], a_vnorm_g[j], a_vnorm_b[j], a_w_out[j])
        else:
            y = dilated_attention_mixer(h, b_w_in[j], b_w_out[j], rel_bias)
        x = x + rms_norm(y, norm_post[i])
    return x
```

```python
import numpy as np
import concourse.bass as bass
import concourse.mybir as mybir
from concourse.bass_utils import run_bass_kernel_spmd

F32 = mybir.dt.float32
BF16 = mybir.dt.bfloat16
AF = mybir.ActivationFunctionType
ALU = mybir.AluOpType
AX = mybir.AxisListType

ENGS = ("pe", "act", "dve", "pool", "sp")


class Sched:
    def __init__(self, nc, same_engine_sync=True, tag=""):
        self.nc = nc
        self.tag = tag
        self.ops = {e: [] for e in ENGS}
        self.last_write = {}
        self.readers = {}
        self.clock = {e: {} for e in ENGS}
        self.ev_clock = {}
        self.dma_cnt = {}
        self.same = same_engine_sync
        self.targets = set()

    def _need(self, eng, ev, waits):
        name, idx = ev
        if not self.same and name == eng:
            return
        ck = self.clock[eng]
        if ck.get(name, 0) >= idx:
            return
        waits.append(ev)
        self.targets.add(ev)
        for k, v in self.ev_clock.get(ev, {}).items():
            if ck.get(k, 0) < v:
                ck[k] = v
        ck[name] = idx

    def op(self, eng, fn, r=(), w=(), dma=None):
        waits = []
        for key in r:
            ev = self.last_write.get(key)
            if ev is not None and not (ev[0] == eng and eng == "pe"):
                self._need(eng, ev, waits)
        for key in w:
            ev = self.last_write.get(key)
            if ev is not None and not (ev[0] == eng and eng == "pe"):
                self._need(eng, ev, waits)
            for name, idx in self.readers.get(key, {}).items():
                if not (name == eng and eng == "pe"):
                    self._need(eng, (name, idx), waits)
        idx = len(self.ops[eng]) + 1
        if dma is not None:
            c = self.dma_cnt.get(dma, 0) + 1
            self.dma_cnt[dma] = c
            ev = ("d:" + dma, c)
        else:
            ev = (eng, idx)
        snap = dict(self.clock[eng])
        self.ev_clock[ev] = snap
        for key in r:
            self.readers.setdefault(key, {})
            cur = self.readers[key].get(ev[0], 0)
            if ev[1] > cur:
                self.readers[key][ev[0]] = ev[1]
        for key in w:
            self.last_write[key] = ev
            self.readers[key] = {}
        self.ops[eng].append(dict(fn=fn, waits=waits, ev=ev, dma=dma))
        return ev

    def barrier(self):
        evs = []
        for e in ENGS:
            n = 0
            for i, o in enumerate(self.ops[e]):
                if o["ev"] is not None and o["ev"][0] == e:
                    n = i + 1
            if n:
                evs.append((e, n))
        for k, v in self.dma_cnt.items():
            evs.append(("d:" + k, v))
        for e in ENGS:
            self.wait_all(e, [ev for ev in evs if not (ev[0] == e)])

    def wait_all(self, eng, evs):
        waits = []
        for ev in evs:
            self._need(eng, ev, waits)
        self.ops[eng].append(dict(fn=None, waits=waits, ev=None, dma=None))

    def emit(self):
        nc = self.nc
        esem = {e: nc.alloc_semaphore(name=self.tag + "s_" + e) for e in ENGS}
        dsem = {n: nc.alloc_semaphore(name=self.tag + "d_" + n) for n in self.dma_cnt}
        tval = {}
        for e in ENGS:
            c = 0
            for i, o in enumerate(self.ops[e]):
                ev = o["ev"]
                if ev is not None and ev[0] == e and ev in self.targets:
                    c += 1
                    tval[ev] = c
            assert c < 60000, (e, c)

        def run(e, h):
            for o in self.ops[e]:
                for (name, idx) in o["waits"]:
                    if name.startswith("d:"):
                        h.wait_ge(dsem[name[2:]], 16 * idx)
                    else:
                        h.wait_ge(esem[name], tval[(name, idx)])
                if o["fn"] is None:
                    continue
                ins = o["fn"](h)
                if o["dma"] is not None:
                    ins.then_inc(dsem[o["dma"]], 16)
                elif o["ev"] in self.targets:
                    ins.then_inc(esem[e], 1)

        with nc.Block() as block:
            @block.tensor
            def _(h):
                run("pe", h)

            @block.scalar
            def _(h):
                run("act", h)

            @block.vector
            def _(h):
                run("dve", h)

            @block.gpsimd
            def _(h):
                run("pool", h)

            @block.sync
            def _(h):
                run("sp", h)


D = 1024
AW = 2048
EPS = 1e-6


def bc_ap(t, off, n):
    return bass.AP(t, off, [[0, 128], [1, n]])


class Ctx:
    pass


def setup_common(nc, S, C):
    if not hasattr(C, "alloc"):
        C.alloc = nc.alloc_sbuf_tensor
    C.identf = nc.alloc_sbuf_tensor("identf", [128, 128], F32)
    C.ident = nc.alloc_sbuf_tensor("ident", [128, 128], BF16)
    identf, ident = C.identf, C.ident
    S.op("pool", lambda h: h.memset(identf[:], 0.0), w=["identf"])
    S.op("pool", lambda h: h.affine_select(out=identf[:], in_=identf[:], pattern=[[-1, 128]],
                                           compare_op=ALU.not_equal, fill=1.0, base=0,
                                           channel_multiplier=1), r=["identf"], w=["identf"])
    S.op("dve", lambda h: h.tensor_copy(out=ident[:], in_=identf[:]), r=["identf"], w=["ident"])
    C.ps = [nc.alloc_psum_tensor("ps%d" % i, [128, 512], F32) for i in range(8)]


def rstd_ops(S, ssq_ap, out_ap, n, rkeys, wkey):
    S.op("dve", lambda h: h.tensor_scalar(out=out_ap, in0=ssq_ap, scalar1=1.0 / n, scalar2=EPS,
                                          op0=ALU.mult, op1=ALU.add), r=rkeys, w=[wkey])
    S.op("act", lambda h: h.activation(out=out_ap, in_=out_ap, func=AF.Sqrt), r=[wkey], w=[wkey])
    S.op("dve", lambda h: h.reciprocal(out=out_ap, in_=out_ap), r=[wkey], w=[wkey])


def phase_a(nc, S, C, x_in, x_out, nch, T):
    ps = C.ps
    ident = C.ident
    w_in = C.alloc("a_win", [128, 8, 6144], BF16)
    w_out = C.alloc("a_wout", [128, 16, 1024], BF16)
    w_sT = C.alloc("a_wsT", [128, 16, 128], BF16)
    bsT = C.alloc("a_bsT", [128, 16], F32)
    gpre = C.alloc("a_gpre", [128, 1024], F32)
    gpost = C.alloc("a_gpost", [128, 1024], F32)
    vng = C.alloc("a_vng", [128, 2048], F32)
    vnb = C.alloc("a_vnb", [128, 2048], F32)
    win_d = T["a_w_in"].ap().rearrange("(k p) n -> p k n", p=128)
    wout_d = T["a_w_out"].ap().rearrange("(k p) n -> p k n", p=128)
    nb_order = [4, 5, 6, 7, 0, 1, 2, 3, 8, 9, 10, 11]
    S.op("sp", lambda h: h.dma_start(out=gpre[:], in_=bc_ap(T["norm_pre"], 0, 1024)), w=["gpre"], dma="c_gpre")
    for nb in nb_order:
        S.op("pool", lambda h, nb=nb: h.dma_start(out=w_in[:, :, nb * 512:(nb + 1) * 512],
                                                  in_=win_d[:, :, nb * 512:(nb + 1) * 512]),
             w=[("win", nb)], dma="c_win%d" % nb)
        if nb == 7:
            S.op("pool", lambda h: h.dma_start(out=w_sT[:], in_=T["a_w_sT"].ap()), w=["wsT"], dma="c_wsT")
    for nb in range(2):
        S.op("pool", lambda h, nb=nb: h.dma_start(out=w_out[:, :, nb * 512:(nb + 1) * 512],
                                                  in_=wout_d[:, :, nb * 512:(nb + 1) * 512]),
             w=[("wout", nb)], dma="c_wout%d" % nb)
    S.op("sp", lambda h: h.dma_start(out=vng[:], in_=bc_ap(T["a_vnorm_g"], 0, 2048)), w=["vng"], dma="c_vng")
    S.op("sp", lambda h: h.dma_start(out=vnb[:], in_=bc_ap(T["a_vnorm_b"], 0, 2048)), w=["vnb"], dma="c_vnb")
    S.op("sp", lambda h: h.dma_start(out=bsT[:], in_=T["a_b_sT"].ap()), w=["bsT"], dma="c_bsT")
    S.op("sp", lambda h: h.dma_start(out=gpost[:], in_=bc_ap(T["norm_post"], 0, 1024)), w=["gpost"], dma="c_gpost")

    NX = 2
    xs = [C.alloc("a_xs%d" % i, [128, 1024], F32) for i in range(NX)]
    hb = [C.alloc("a_hb%d" % i, [128, 1024], BF16) for i in range(2)]
    hT = [C.alloc("a_hT%d" % i, [128, 8, 128], BF16) for i in range(2)]
    st = [C.alloc("a_st%d" % i, [128, 8], F32) for i in range(NX)]
    junk = C.alloc("a_junk", [128, 1024], BF16)
    u = C.alloc("a_u", [128, 2048], BF16)
    sg = C.alloc("a_sg", [128, 2048], BF16)
    gv = C.alloc("a_gv", [128, 2048], F32)
    ot = C.alloc("a_ot", [128, 2048], F32)
    v = C.alloc("a_v", [128, 2048], BF16)
    us = u
    y = u
    yT = C.alloc("a_yT", [128, 16, 128], BF16)
    bst = C.alloc("a_bst", [128, 4, 6], F32)
    mv = C.alloc("a_mv", [128, 4], F32)
    zrot = [0]

    def zbank():
        b = zrot[0] % 4
        zrot[0] += 1
        return b

    def LOAD(i):
        s = i % NX
        S.op("sp", lambda h: h.dma_start(out=xs[s][:], in_=x_in[i * 128:(i + 1) * 128, :]),
             w=[("xs", s)], dma="a_x%d" % s)
        S.op("act", lambda h: h.activation(out=junk[:], in_=xs[s][:], func=AF.Square, accum_out=st[s][:, 0:1]),
             r=[("xs", s)], w=["junk", ("st0", s)])
        rstd_ops(S, st[s][:, 0:1], st[s][:, 1:2], 1024, [("st0", s)], ("st1", s))
        S.op("dve", lambda h: h.scalar_tensor_tensor(out=hb[i % 2][:], in0=xs[s][:], scalar=st[s][:, 1:2],
                                                     in1=gpre[:], op0=ALU.mult, op1=ALU.mult),
             r=[("xs", s), ("st1", s), "gpre"], w=[("hb", i % 2)])

    def TH(i):
        b = zbank() if i > 0 else 6
        pb = ps[b][:].bitcast(BF16)
        for k in range(8):
            S.op("pe", lambda h, k=k: h.transpose(pb[:, k * 128:(k + 1) * 128], hb[i % 2][:, k * 128:(k + 1) * 128], ident[:]),
                 r=[("hb", i % 2), "ident"], w=[("ps", b)])
        S.op("act", lambda h: h.activation(out=hT[i % 2][:].rearrange("p k t -> p (k t)"), in_=pb[:, 0:1024], func=AF.Copy),
             r=[("ps", b)], w=[("hT", i % 2)])

    def Z(i):
        hTi = hT[i % 2]
        for nb in nb_order:
            b = zbank()
            for k in range(8):
                S.op("pe", lambda h, k=k, nb=nb, b=b: h.matmul(ps[b][:, :], lhsT=hTi[:, k, :],
                                                               rhs=w_in[:, k, nb * 512:(nb + 1) * 512],
                                                               start=(k == 0), stop=(k == 7)),
                     r=[("hT", i % 2), ("win", nb)], w=[("ps", b)])
            if nb < 4:
                S.op("act", lambda h, nb=nb, b=b: h.activation(out=u[:, nb * 512:(nb + 1) * 512], in_=ps[b][:, :],
                                                               func=AF.Gelu_apprx_tanh),
                     r=[("ps", b)], w=[("u", nb)])
            elif nb < 8:
                j = nb - 4
                S.op("act", lambda h, j=j, b=b: h.activation(out=gv[:, j * 512:(j + 1) * 512], in_=ps[b][:, :],
                                                             func=AF.Gelu_apprx_tanh),
                     r=[("ps", b)], w=[("gv", j)])
                S.op("dve", lambda h, j=j: h.bn_stats(out=bst[:, j, :], in_=gv[:, j * 512:(j + 1) * 512]),
                     r=[("gv", j)], w=[("bst", j)])
                if j == 3:
                    LN(i)
                    if i + 1 < nch:
                        LOAD(i + 1)
            else:
                j = nb - 8
                S.op("act", lambda h, j=j, b=b: h.activation(out=sg[:, j * 512:(j + 1) * 512], in_=ps[b][:, :],
                                                             func=AF.Silu),
                     r=[("ps", b)], w=[("sg", j)])

    def LN(i):
        S.op("dve", lambda h: h.bn_aggr(out=mv[:, 0:2], in_=bst[:].rearrange("p a b -> p (a b)")),
             r=[("bst", j) for j in range(4)], w=["mv"])
        S.op("dve", lambda h: h.tensor_scalar(out=mv[:, 2:3], in0=mv[:, 1:2], scalar1=1.0, scalar2=EPS, op0=ALU.mult, op1=ALU.add),
             r=["mv"], w=["mv2"])
        S.op("act", lambda h: h.activation(out=mv[:, 2:3], in_=mv[:, 2:3], func=AF.Sqrt), r=["mv2"], w=["mv2"])
        S.op("dve", lambda h: h.reciprocal(out=mv[:, 2:3], in_=mv[:, 2:3]), r=["mv2"], w=["mv2"])
        S.op("dve", lambda h: h.tensor_scalar(out=gv[:], in0=gv[:], scalar1=mv[:, 0:1], scalar2=mv[:, 2:3],
                                              op0=ALU.subtract, op1=ALU.mult),
             r=["mv", "mv2"] + [("gv", j) for j in range(4)], w=[("gv", j) for j in range(4)])
        S.op("pool", lambda h: h.tensor_tensor(out=gv[:], in0=gv[:], in1=vng[:], op=ALU.mult),
             r=["vng"] + [("gv", j) for j in range(4)], w=[("gv", j) for j in range(4)])
        S.op("pool", lambda h: h.tensor_tensor(out=v[:], in0=gv[:], in1=vnb[:], op=ALU.add),
             r=["vnb"] + [("gv", j) for j in range(4)], w=["v"])

    def SP(i):
        for hh in range(16):
            b = 4 + hh // 4
            S.op("pe", lambda h, hh=hh, b=b: h.matmul(ps[b][:, (hh % 4) * 128:(hh % 4 + 1) * 128], lhsT=w_sT[:, hh, :],
                                                      rhs=v[:, hh * 128:(hh + 1) * 128], start=True, stop=True),
                 r=["wsT", "v"], w=[("ps", b)])
            if hh % 4 == 3:
                for h2 in range(hh - 3, hh + 1):
                    if h2 % 4 == 0:
                        jq = h2 // 4
                        S.op("pool", lambda h, jq=jq: h.tensor_tensor(out=us[:, jq * 512:(jq + 1) * 512],
                                                                      in0=u[:, jq * 512:(jq + 1) * 512],
                                                                      in1=sg[:, jq * 512:(jq + 1) * 512], op=ALU.mult),
                             r=[("u", jq), ("sg", jq)], w=[("us", jq), ("u", jq)])
                    S.op("dve", lambda h, h2=h2, b=b: h.scalar_tensor_tensor(
                        out=y[:, h2 * 128:(h2 + 1) * 128], in0=ps[b][:, (h2 % 4) * 128:(h2 % 4 + 1) * 128],
                        scalar=bsT[:, h2:h2 + 1], in1=us[:, h2 * 128:(h2 + 1) * 128], op0=ALU.add, op1=ALU.mult),
                         r=[("ps", b), "bsT", ("us", h2 // 4)], w=[("y", h2 // 8), ("u", h2 // 4), ("us", h2 // 4)])

    def TY(i):
        for half in range(2):
            b = 6 + half
            pb = ps[b][:].bitcast(BF16)
            for k in range(8):
                c = half * 8 + k
                S.op("pe", lambda h, k=k, c=c, pb=pb: h.transpose(pb[:, k * 128:(k + 1) * 128], y[:, c * 128:(c + 1) * 128], ident[:]),
                     r=[("y", half), "ident"] + [("u", half * 2), ("u", half * 2 + 1)], w=[("ps", b)])
            S.op("act", lambda h, half=half, pb=pb: h.activation(out=yT[:, half * 8:(half + 1) * 8, :].rearrange("p k t -> p (k t)"),
                                                          in_=pb[:, 0:1024], func=AF.Copy),
                 r=[("ps", b)], w=[("yT", half)])

    obanks = {}

    def O_mm(i):
        s = i % NX
        bs = []
        for nb in range(2):
            b = zbank()
            bs.append(b)
            for k in range(16):
                S.op("pe", lambda h, k=k, nb=nb, b=b: h.matmul(ps[b][:, :], lhsT=yT[:, k, :],
                                                               rhs=w_out[:, k, nb * 512:(nb + 1) * 512],
                                                               start=(k == 0), stop=(k == 15)),
                     r=[("yT", k // 8), ("wout", nb)], w=[("ps", b)])
            S.op("act", lambda h, nb=nb, b=b: h.activation(out=junk[:, 0:512], in_=ps[b][:, :], func=AF.Square,
                                                           accum_out=st[s][:, 2 + nb:3 + nb]),
                 r=[("ps", b)], w=["junk", ("st2", s, nb)])
        obanks[i] = bs

    def O_post(i):
        s = i % NX
        bs = obanks[i]
        S.op("dve", lambda h: h.tensor_tensor(out=st[s][:, 4:5], in0=st[s][:, 2:3], in1=st[s][:, 3:4], op=ALU.add),
             r=[("st2", s, 0), ("st2", s, 1)], w=[("st4", s)])
        rstd_ops(S, st[s][:, 4:5], st[s][:, 5:6], 1024, [("st4", s)], ("st5", s))
        S.op("sp", lambda h: h.dma_start(out=ot[:, 1024:2048], in_=x_in[i * 128:(i + 1) * 128, :]),
             w=[("ot", 2), ("ot", 3)], dma="a_xr")
        for nb in range(2):
            b = bs[nb]
            S.op("dve", lambda h, nb=nb, b=b: h.scalar_tensor_tensor(
                out=ot[:, nb * 512:(nb + 1) * 512], in0=ps[b][:, :], scalar=st[s][:, 5:6],
                in1=gpost[:, nb * 512:(nb + 1) * 512], op0=ALU.mult, op1=ALU.mult),
                 r=[("ps", b), ("st5", s), "gpost"], w=[("ot", nb)])
            S.op("pool", lambda h, nb=nb: h.tensor_tensor(out=ot[:, nb * 512:(nb + 1) * 512], in0=ot[:, nb * 512:(nb + 1) * 512],
                                                          in1=ot[:, 1024 + nb * 512:1024 + (nb + 1) * 512], op=ALU.add),
                 r=[("ot", nb), ("ot", 2 + nb)], w=[("ot", nb)])
        S.op("sp", lambda h: h.dma_start(out=x_out[i * 128:(i + 1) * 128, :], in_=ot[:, 0:1024]),
             r=[("ot", 0), ("ot", 1)], w=[("x1", i)], dma="a_o")

    LOAD(0)
    TH(0)
    for i in range(nch):
        Z(i)
        SP(i)
        if i + 1 < nch:
            TH(i + 1)
        if i >= 1:
            O_mm(i - 1)
            O_post(i - 1)
        TY(i)
    O_mm(nch - 1)
    O_post(nch - 1)


def declare_inputs(nc, names):
    shapes = {
        "norm_pre": [2 * 1024], "norm_post": [2 * 1024],
        "a_w_in": [1024, 6144], "a_w_sT": [128, 16 * 128], "a_b_sT": [128, 16],
        "a_vnorm_g": [2048], "a_vnorm_b": [2048], "a_w_out": [2048, 1024],
        "b_w_in": [1024, 10240], "b_w_out": [1024, 1024],
    }
    return {n: nc.dram_tensor(n, shapes[n], F32, kind="ExternalInput") for n in names}


A_NAMES = ["norm_pre", "norm_post", "a_w_in", "a_w_sT", "a_b_sT", "a_vnorm_g", "a_vnorm_b", "a_w_out"]


def build_a(nch):
    nc = bass.Bass("TRN2", target_bir_lowering=False)
    T = declare_inputs(nc, A_NAMES)
    x_in = nc.dram_tensor("x_in", [nch * 128, 1024], F32, kind="ExternalInput").ap()
    x_out = nc.dram_tensor("x1", [nch * 128, 1024], F32, kind="ExternalOutput").ap()
    S = Sched(nc)
    C = Ctx()
    setup_common(nc, S, C)
    phase_a(nc, S, C, x_in, x_out, nch, T)
    S.wait_all("sp", [("d:" + k, v) for k, v in S.dma_cnt.items() if k.startswith("a_o")])
    S.emit()
    return nc


def host_consts(inp):
    f = lambda a: np.ascontiguousarray(np.asarray(a, dtype=np.float32))
    c = {
        "norm_pre": f(inp["norm_pre"]).reshape(-1), "norm_post": f(inp["norm_post"]).reshape(-1),
        "a_w_in": f(inp["a_w_in"])[0], "a_w_out": f(inp["a_w_out"])[0],
        "a_w_sT": f(np.transpose(f(inp["a_w_s"])[0], (2, 0, 1))).reshape(128, 16 * 128),
        "a_b_sT": f(f(inp["a_b_s"])[0].T),
        "a_vnorm_g": f(inp["a_vnorm_g"])[0], "a_vnorm_b": f(inp["a_vnorm_b"])[0],
        "b_w_in": f(inp["b_w_in"])[0], "b_w_out": f(inp["b_w_out"])[0],
    }
    return c


GROUPS = ((1, 17), (4, 5), (16, 2))
VT_OFF = (0, 17, 37)


def phase_b(nc, S, C, x1e, out_d, T, stage=99):
    ps = C.ps
    ident = C.ident
    identf = C.identf
    gpre = C.alloc("b_gpre", [128, 1024], F32)
    gpost = C.alloc("b_gpost", [128, 1024], F32)
    tab = C.alloc("b_tab", [33, 48], F32)
    Jf = C.alloc("b_Jf", [128, 128], F32)
    J = C.alloc("b_J", [128, 128], BF16)
    vm = C.alloc("b_vm", [128, 69], F32)
    rbcf = [C.alloc("b_rbc%d" % i, [128, 2048], F32) for i in range(2)]
    rbc = [t[0:64] for t in rbcf]
    rdram = nc.dram_tensor("b_rdram", [4 * 2048], F32)
    r16 = C.alloc("b_r16", [128, 2, 16], F32)

    tvec = nc.dram_tensor("b_tvec", [3 * 16 * 512], BF16)
    hT = C.alloc("b_hT", [128, 8, 4096], BF16)
    yTa = C.alloc("b_yT", [128, 8, 2048], BF16)
    scr = C.alloc("b_scr", [128, 4096], F32)
    vtall = C.alloc("b_vtall", [128, 69 * 130], BF16)
    vt = [vtall[:, VT_OFF[g] * 130:(VT_OFF[g] + GROUPS[g][0] * GROUPS[g][1]) * 130].rearrange("p (t h c) -> p t h c", h=2, c=65)
          for g in range(3)]
    wout = vtall[:, 0:8192].rearrange("p (k n) -> p k n", k=8)
    QT = C.alloc("b_QT", [128, 2048], BF16)
    KT = C.alloc("b_KT", [128, 4096], BF16)
    gsil = C.alloc("b_gsil", [128, 2, 2048], BF16)[0:64]
    ohs = KT[0:33, 0:1536].rearrange("p (g n) -> p g n", g=3)
    tvs = QT[0:16, 0:1536].rearrange("p (g n) -> p g n", g=3)
    wqkv = [C.alloc("b_wqkv%d" % i, [128, 3, 8, 128], BF16) for i in range(2)]
    wg = [C.alloc("b_wg%d" % i, [128, 8, 128], BF16) for i in range(2)]
    es = [C.alloc("b_es%d" % i, [128, 512], BF16) for i in range(2)]
    pt = [C.alloc("b_pt%d" % i, [128, 512], BF16) for i in range(4)]
    h2 = [C.alloc("b_h2%d" % i, [128, 512], BF16) for i in range(2)]
    eb = [C.alloc("b_eb%d" % i, [128, 512], BF16) for i in range(2)]
    st = C.alloc("b_st", [128, 4, 8], F32)
    win_d = T["b_w_in"].ap().rearrange("(k p) n -> p k n", p=128)

    S.op("sp", lambda h: h.dma_start(out=gpre[:], in_=bc_ap(T["norm_pre"], 1024, 1024)), w=["gpre"], dma="b_c0")
    S.op("dve", lambda h: h.memset(st[:, 0, 6:8], 0.0), w=["stz", "stx"])
    S.op("sp", lambda h: h.dma_start(out=gpost[:], in_=bc_ap(T["norm_post"], 1024, 1024)), w=["gpost"], dma="b_c1")
    S.op("sp", lambda h: h.dma_start(out=vm[:], in_=T["vmask"].ap()), w=["vm"], dma="b_c2")
    S.op("sp", lambda h: h.dma_start(out=ohs, in_=T["oh"].ap().rearrange("g b n -> b g n")), w=["ohs"], dma="b_c3")
    S.op("dve", lambda h: h.memset(tab[:], -30000.0), w=["tab"])
    S.op("sp", lambda h: h.dma_start(out=tab[0:32, :], in_=T["rel_bias"].ap()), w=["tab"], dma="b_c4")
    S.op("pool", lambda h: h.memset(Jf[:], 0.0), w=["Jf"])
    S.op("pool", lambda h: h.affine_select(out=Jf[:], in_=Jf[:], pattern=[[1, 128]], compare_op=ALU.not_equal,
                                           fill=1.0, base=-127, channel_multiplier=1), r=["Jf"], w=["Jf"])
    S.op("dve", lambda h: h.tensor_copy(out=J[:], in_=Jf[:]), r=["Jf"], w=["J"])
    tabh = C.alloc("b_tabh", [33, 48], BF16)
    tabl = C.alloc("b_tabl", [33, 48], BF16)
    S.op("dve", lambda h: h.tensor_copy(out=tabh[:], in_=tab[:]), r=["tab"], w=["tabh"])
    S.op("dve", lambda h: h.tensor_tensor(out=tabl[:], in0=tab[:], in1=tabh[:], op=ALU.subtract), r=["tab", "tabh"], w=["tabl"])
    for g in range(3):
        S.op("pe", lambda h, g=g: h.matmul(ps[0][0:16, :], lhsT=tabh[0:33, g * 16:(g + 1) * 16], rhs=ohs[0:33, g, :],
                                           start=True, stop=False), r=["tabh", "ohs"], w=[("ps", 0)])
        S.op("pe", lambda h, g=g: h.matmul(ps[0][0:16, :], lhsT=tabl[0:33, g * 16:(g + 1) * 16], rhs=ohs[0:33, g, :],
                                           start=False, stop=True), r=["tabl", "ohs"], w=[("ps", 0)])
        S.op("act", lambda h, g=g: h.activation(out=tvs[:, g, :], in_=ps[0][0:16, :], func=AF.Exp),
             r=[("ps", 0)], w=[("tvs", g)])
        S.op("sp", lambda h, g=g: h.dma_start(out=bass.AP(tvec, g * 16 * 512, [[512, 16], [1, 512]]), in_=tvs[:, g, :]),
             r=[("tvs", g)], w=[("tvec", g)], dma="b_tv%d" % g)
    for g in range(3):
        nt = GROUPS[g][0] * GROUPS[g][1]
        for hh in range(2):
            S.op("dve", lambda h, g=g, hh=hh, nt=nt: h.tensor_copy(out=vt[g][:, :, hh, 64], in_=vm[:, VT_OFF[g]:VT_OFF[g] + nt]),
                 r=["vm"], w=[("vtv", g)])

    NB1 = 4
    xs = [scr[:, i * 1024:(i + 1) * 1024] for i in range(NB1)]
    hbv = rbcf[0][:, :].bitcast(BF16)
    hb = [hbv[:, i * 1024:(i + 1) * 1024] for i in range(NB1)]
    junk = rbcf[1][:, 0:512].bitcast(BF16)

    def LOADB(i):
        s = i % NB1
        S.op("sp", lambda h: h.dma_start(out=xs[s], in_=x1e[i * 128:(i + 1) * 128, :]), w=[("xs", s)], dma="b_x%d" % s)
        S.op("act", lambda h: h.activation(out=junk, in_=xs[s], func=AF.Square, accum_out=st[:, s, 0:1]),
             r=[("xs", s)], w=["junk", ("st0", s)])
        rstd_ops(S, st[:, s, 0:1], st[:, s, 1:2], 1024, [("st0", s)], ("st1", s))
        S.op("dve", lambda h: h.scalar_tensor_tensor(out=hb[s], in0=xs[s], scalar=st[:, s, 1:2], in1=gpre[:],
                                                     op0=ALU.mult, op1=ALU.mult),
             r=[("xs", s), ("st1", s), "gpre"], w=[("hb", s)])

    def THB(i):
        s = i % NB1
        b = 4 + s
        pb = ps[b][:].bitcast(BF16)
        for k in range(8):
            S.op("pe", lambda h, k=k: h.transpose(pb[:, k * 128:(k + 1) * 128], hb[s][:, k * 128:(k + 1) * 128], ident[:]),
                 r=[("hb", s), "ident"], w=[("ps", b)])
        S.op("act" if i % 2 else "dve",
             (lambda h: h.activation(out=hT[:, :, i * 128:(i + 1) * 128], in_=pb[:, 0:1024].rearrange("p (k t) -> p k t", k=8), func=AF.Copy))
             if i % 2 else
             (lambda h: h.tensor_copy(out=hT[:, :, i * 128:(i + 1) * 128], in_=pb[:, 0:1024].rearrange("p (k t) -> p k t", k=8))),
             r=[("ps", b)], w=[("hT", i // 4)])

    if stage < 1:
        return
    for i in range(NB1 - 1):
        LOADB(i)
    for i in range(32):
        if i + NB1 - 1 < 32:
            LOADB(i + NB1 - 1)
        THB(i)
    S.barrier()
    if stage < 2:
        return

    acc = [scr[:, 0:2048], scr[:, 2048:4096]]
    rot = {"p": 0, "s": 0, "o": 0, "e": 0}

    def pbank():
        rot["p"] += 1
        return rot["p"] % 2

    def load_w(hp, g):
        sl = (hp * 3 + g) % 2
        for w3 in range(3):
            c0 = g * 3072 + w3 * 1024 + hp * 128
            S.op("pool", lambda h, w3=w3, c0=c0: h.dma_start(out=wqkv[sl][:, w3, :, :], in_=win_d[:, :, c0:c0 + 128]),
                 w=[("wqkv", sl, w3)], dma="b_w%d_%d" % (sl, w3))
        if g == 0:
            c0 = 9216 + hp * 128
            S.op("pool", lambda h, c0=c0: h.dma_start(out=wg[hp % 2][:], in_=win_d[:, :, c0:c0 + 128]),
                 w=[("wg", hp % 2)], dma="b_wg%d" % (hp % 2))
        e = (hp * 3 + g) % 2
        off = ((g * 16 + hp * 2) * 2) * 256
        S.op("sp", lambda h: h.dma_start(out=h2[e][:].rearrange("p (a i) -> p a i", a=4),
                                         in_=bass.AP(tvec, off, [[1, 128], [256, 4], [1, 128]])),
             r=[("tvec", g)], w=[("h2", e)], dma="b_h2%d" % e)

    def make_eb(hp, g):
        e = (hp * 3 + g) % 2
        b = pbank()
        S.op("pe", lambda h: h.matmul(ps[b][:, :], lhsT=J[:], rhs=h2[e][:], start=True, stop=True),
             r=["J", ("h2", e)], w=[("ps", b)])
        S.op("dve", lambda h: h.tensor_copy(out=eb[e][:], in_=ps[b][:, :]), r=[("ps", b)], w=[("eb", e)])

    def proj(hp, g, parts="qkv"):
        sl = (hp * 3 + g) % 2
        d, ntr = GROUPS[g]
        if "q" in parts:
            for blk in range(4):
                b = pbank()
                for k in range(8):
                    S.op("pe", lambda h, k=k, blk=blk, b=b: h.matmul(ps[b][:, :], lhsT=wqkv[sl][:, 0, k, :],
                                                                     rhs=hT[:, k, 1024 + blk * 512:1024 + (blk + 1) * 512],
                                                                     start=(k == 0), stop=(k == 7)),
                         r=[("wqkv", sl, 0)], w=[("ps", b)])
                S.op("act", lambda h, blk=blk, b=b: h.activation(out=QT[:, blk * 512:(blk + 1) * 512], in_=ps[b][:, :],
                                                                 func=AF.Copy, scale=0.125),
                     r=[("ps", b)], w=[("QT", blk)])
        if "k" in parts:
            kstart, nkb = ((896, 5), (768, 5), (0, 8))[g]
            for j in range(nkb):
                c0 = kstart + j * 512
                b = pbank()
                for k in range(8):
                    S.op("pe", lambda h, k=k, c0=c0, b=b: h.matmul(ps[b][:, :], lhsT=wqkv[sl][:, 1, k, :],
                                                                   rhs=hT[:, k, c0:c0 + 512],
                                                                   start=(k == 0), stop=(k == 7)),
                         r=[("wqkv", sl, 1)], w=[("ps", b)])
                S.op("act", lambda h, c0=c0, b=b: h.activation(out=KT[:, c0:c0 + 512], in_=ps[b][:, :], func=AF.Copy),
                     r=[("ps", b)], w=[("KT", j)])
        if "v" in parts:
            for u in v_units(hp, g):
                u()

    def v_units(hp, g):
        sl = (hp * 3 + g) % 2
        d, ntr = GROUPS[g]
        base = 1024 // d - 64
        nt = d * ntr
        units = []
        TPU = 2
        for t0 in range(0, nt, TPU):
            def unit(t0=t0):
                b = pbank()
                n = min(TPU, nt - t0)
                for tt in range(n):
                    idx = t0 + tt
                    r, j = idx // ntr, idx % ntr
                    e0 = (base + 128 * j) * d + r
                    for k in range(8):
                        S.op("pe", lambda h, k=k, tt=tt, e0=e0, b=b: h.matmul(
                            ps[b][:, tt * 128:(tt + 1) * 128],
                            lhsT=hT[:, k, e0:e0 + 127 * d + 1:d], rhs=wqkv[sl][:, 2, k, :],
                            start=(k == 0), stop=(k == 7)),
                             r=[("wqkv", sl, 2)], w=[("ps", b)])
                S.op("act", lambda h, t0=t0, n=n, b=b: h.activation(
                    out=vt[g][:, t0:t0 + n, :, 0:64],
                    in_=ps[b][:, 0:n * 128].rearrange("p (t h c) -> p t h c", t=n, h=2), func=AF.Copy),
                     r=[("ps", b)], w=[("vt", g)])
            units.append(unit)
        return units

    def gate(hp):
        for blk in range(4):
            b = pbank()
            for k in range(8):
                S.op("pe", lambda h, k=k, blk=blk, b=b: h.matmul(ps[b][:, :], lhsT=wg[hp % 2][:, k, :],
                                                                 rhs=hT[:, k, 1024 + blk * 512:1024 + (blk + 1) * 512],
                                                                 start=(k == 0), stop=(k == 7)),
                     r=[("wg", hp % 2)], w=[("ps", b)])
            for hh in range(2):
                S.op("act", lambda h, blk=blk, b=b, hh=hh: h.activation(
                    out=gsil[:, hh, blk * 512:(blk + 1) * 512], in_=ps[b][hh * 64:(hh + 1) * 64, :], func=AF.Silu),
                     r=[("ps", b)], w=[("gsil", blk)])

    def attn(hp, g, fillers=()):
        fillers = list(fillers)
        d, ntr = GROUPS[g]
        e = (hp * 3 + g) % 2
        base = 1024 // d - 64
        qs = 1024 // d
        if g == 0:
            banks = [[(0, 4 * tb + q) for q in range(4)] for tb in range(4)]
        elif g == 1:
            banks = [[(r, q) for q in range(4)] for r in range(4)]
        else:
            banks = [[(4 * rb + q, 0) for q in range(4)] for rb in range(4)]
        units = [(bi, qi, r, t) for bi, bk in enumerate(banks) for qi, (r, t) in enumerate(bk)]

        def ST(p):
            bk = (2, 3) if rot["s"] % 2 == 0 else (4, 5)
            rot["s"] += 1
            for qq in range(2):
                bi, qi, r, t = units[2 * p + qq]
                q0 = (qs + 128 * t) * d + r - 1024
                for ab in range(2):
                    k0 = (base + 128 * (t + ab)) * d + r
                    col = (qq * 2 + ab) * 128
                    for hh in range(2):
                        S.op("pe", lambda h, hh=hh, k0=k0, col=col, q0=q0, bk=bk: h.matmul(
                            ps[bk[hh]][:, col:col + 128],
                            lhsT=KT[hh * 64:(hh + 1) * 64, k0:k0 + 127 * d + 1:d],
                            rhs=QT[hh * 64:(hh + 1) * 64, q0:q0 + 127 * d + 1:d], start=True, stop=True),
                             r=[("KT", kb) for kb in range(8)] + [("QT", qb) for qb in range(4)], w=[("ps", bk[hh])])
            return bk

        def EP(p, bk):
            xs_ = []
            for hh in range(2):
                x = hh * 2 + (rot["e"] % 2)
                xs_.append(x)
                ebh = bass.AP(eb[e], hh * 256, [[512, 128], [0, 2], [1, 256]])
                S.op("act", lambda h, hh=hh, bk=bk: h.activation(out=es[hh][:], in_=ps[bk[hh]][:, :], func=AF.Exp),
                     r=[("ps", bk[hh])], w=[("es", hh)])
                S.op("dve", lambda h, hh=hh, x=x, ebh=ebh: h.tensor_tensor(
                    out=pt[x][:].rearrange("p (q c) -> p q c", q=2), in0=es[hh][:].rearrange("p (q c) -> p q c", q=2),
                    in1=ebh, op=ALU.mult),
                     r=[("es", hh), ("eb", e)], w=[("pt", x)])
            rot["e"] += 1
            return xs_

        def PV(p, xs_):
            for hh in range(2):
                for qq in range(2):
                    bi, qi, r, t = units[2 * p + qq]
                    ob = 6
                    for ab in range(2):
                        idx = r * ntr + t + ab
                        c0 = (qq * 2 + ab) * 128
                        S.op("pe", lambda h, hh=hh, ab=ab, idx=idx, ob=ob, qi=qi, c0=c0, x=xs_[hh]: h.matmul(
                            ps[ob + hh][0:65, qi * 128:(qi + 1) * 128],
                            lhsT=vt[g][:, idx, hh, :], rhs=pt[x][:, c0:c0 + 128],
                            start=(ab == 0), stop=(ab == 1)),
                             r=[("vt", g), ("vtv", g), ("pt", xs_[hh])], w=[("ps", ob + hh)])
            for qq in range(2):
                bi, qi, r, t = units[2 * p + qq]
                ob = 6
                if qi == 3:
                    for hh in range(2):
                        src = ps[ob + hh][0:65, :]
                        if g == 0:
                            S.op("act", lambda h, hh=hh, src=src, bi=bi: h.activation(out=acc[hh][0:65, bi * 512:(bi + 1) * 512], in_=src, func=AF.Copy),
                                 r=[("ps", ob + hh)], w=[("acc", hh)])
                        elif g == 1:
                            dst = acc[hh][0:65, r:r + 511 * 4 + 1:4]
                            S.op("dve", lambda h, dst=dst, src=src: h.tensor_tensor(out=dst, in0=dst, in1=src, op=ALU.add),
                                 r=[("ps", ob + hh), ("acc", hh)], w=[("acc", hh)])
                        else:
                            r0 = r - 3
                            dst = acc[hh][0:65, :].rearrange("p (i r) -> p r i", r=16)[:, r0:r0 + 4, :]
                            S.op("dve", lambda h, dst=dst, src=src: h.tensor_tensor(
                                out=dst, in0=dst, in1=src.rearrange("p (r i) -> p r i", r=4), op=ALU.add),
                                 r=[("ps", ob + hh), ("acc", hh)], w=[("acc", hh)])

        npairs = len(units) // 2
        bks = {0: ST(0)}
        for p in range(npairs):
            if p + 1 < npairs:
                bks[p + 1] = ST(p + 1)
            xs_ = EP(p, bks[p])
            nf = -(-len(fillers) // (npairs - p))
            for _ in range(nf):
                fillers.pop(0)()
            PV(p, xs_)
        for f in fillers:
            f()

    def fin1a(hp):
        for hh in range(2):
            S.op("sp", lambda h, hh=hh: h.dma_start(out=bass.AP(rdram, hh * 2048, [[2048, 1], [1, 2048]]), in_=acc[hh][64:65, :]),
                 r=[("acc", hh)], w=[("rdram", hh)], dma="b_rd%d" % hh)
            S.op("sp", lambda h, hh=hh: h.dma_start(out=r16[:, hh, :], in_=bass.AP(rdram, hh * 2048, [[16, 128], [1, 16]])),
                 r=[("rdram", hh)], w=[("r16", hh)], dma="b_r16%d" % hh)

    def fin1b(hp):
        for hh in range(2):
            S.op("dve", lambda h, hh=hh: h.reciprocal(out=r16[:, hh, :], in_=r16[:, hh, :]), r=[("r16", hh)], w=[("r16", hh)])
            S.op("sp", lambda h, hh=hh: h.dma_start(out=bass.AP(rdram, 4096 + hh * 2048, [[16, 128], [1, 16]]), in_=r16[:, hh, :]),
                 r=[("r16", hh)], w=[("rdram2", hh)], dma="b_rd2%d" % hh)
            S.op("sp", lambda h, hh=hh: h.dma_start(out=rbc[hh][:, :], in_=bass.AP(rdram, 4096 + hh * 2048, [[0, 64], [1, 2048]])),
                 r=[("rdram2", hh)], w=[("rbc", hh)], dma="b_rb%d" % hh)

    def fin_pre(hp):
        for hh in range(2):
            S.op("dve", lambda h, hh=hh: h.tensor_tensor(out=acc[hh][0:64, :], in0=acc[hh][0:64, :], in1=gsil[:, hh, :], op=ALU.mult),
                 r=[("acc", hh)] + [("gsil", blk) for blk in range(4)], w=[("acc", hh)])

    def fin2(hp):
        for hh in range(2):
            S.op("dve", lambda h, hh=hh: h.tensor_tensor(out=yTa[hh * 64:(hh + 1) * 64, hp, :], in0=acc[hh][0:64, :],
                                                         in1=rbc[hh][:, :], op=ALU.mult),
                 r=[("acc", hh), ("rbc", hh)], w=[("yTa", hp)])

    load_w(0, 0)
    seq = [(hp, g) for hp in range(8) for g in range(3)]
    for n_, (hp, g) in enumerate(seq):
        nxt = seq[n_ + 1] if n_ + 1 < len(seq) else None
        if nxt is not None:
            load_w(*nxt)
        make_eb(hp, g)
        first = (n_ == 0)
        if g == 0 and hp > 0:
            fin_pre(hp - 1)
            proj(hp, g, "q")
            fin1b(hp - 1)
            gate(hp)
            fin2(hp - 1)
            S.op("act", lambda h: h.activation(out=st[:, 0, 6:7], in_=st[:, 0, 7:8], func=AF.Exp), r=["stz"], w=["stx"])
            proj(hp, g, "k")
        else:
            proj(hp, g, "qkv" if first else "qk")
            if g == 0:
                gate(hp)
        attn(hp, g, v_units(*nxt) if nxt is not None else ())
        if g == 2:
            fin1a(hp)
    S.op("pool", lambda h: h.dma_start(out=wout, in_=T["b_w_out"].ap().rearrange("(k p) n -> p k n", p=128)),
         w=["wout"] + [("vt", g_) for g_ in range(3)] + [("vtv", g_) for g_ in range(3)], dma="b_wout")
    fin_pre(7)
    fin1b(7)
    fin2(7)
    S.barrier()

    slots = [(scr[:, 0:1024], scr[:, 1024:2048]), (scr[:, 2048:3072], scr[:, 3072:4096]),
             (rbcf[0][:, 0:1024], rbcf[0][:, 1024:2048]), (rbcf[1][:, 0:1024], rbcf[1][:, 1024:2048])]
    for c in range(16):
        sl = c % 4
        o, xr = slots[sl]
        S.op("act", lambda h, c=c, xr=xr: h.dma_start(out=xr, in_=x1e[1024 + c * 128:1024 + (c + 1) * 128, :]),
             w=[("xr", sl)], dma="b_xr%d" % sl)
        bs = []
        for nb in range(2):
            b = 2 * sl + nb
            bs.append(b)
            for k in range(8):
                S.op("pe", lambda h, k=k, nb=nb, b=b, c=c: h.matmul(ps[b][:, :], lhsT=yTa[:, k, c * 128:(c + 1) * 128],
                                                                    rhs=wout[:, k, nb * 512:(nb + 1) * 512],
                                                                    start=(k == 0), stop=(k == 7)),
                     r=["wout"] + [("yTa", k) for k in range(8)], w=[("ps", b)])
            S.op("act", lambda h, nb=nb, b=b, sl=sl: h.activation(out=pt[sl][:], in_=ps[b][:, :], func=AF.Square,
                                                                 accum_out=st[:, sl, 2 + nb:3 + nb]),
                 r=[("ps", b)], w=[("pt", sl), ("st2", sl, nb)])
        S.op("dve", lambda h, sl=sl: h.tensor_tensor(out=st[:, sl, 4:5], in0=st[:, sl, 2:3], in1=st[:, sl, 3:4], op=ALU.add),
             r=[("st2", sl, 0), ("st2", sl, 1)], w=[("st4", sl)])
        rstd_ops(S, st[:, sl, 4:5], st[:, sl, 5:6], 1024, [("st4", sl)], ("st5", sl))
        for nb in range(2):
            b = bs[nb]
            S.op("dve", lambda h, nb=nb, b=b, sl=sl, o=o: h.scalar_tensor_tensor(
                out=o[:, nb * 512:(nb + 1) * 512], in0=ps[b][:, :], scalar=st[:, sl, 5:6],
                in1=gpost[:, nb * 512:(nb + 1) * 512], op0=ALU.mult, op1=ALU.mult),
                 r=[("ps", b), ("st5", sl), "gpost"], w=[("o", sl, nb)])
            S.op("pool", lambda h, nb=nb, o=o, xr=xr: h.tensor_tensor(out=o[:, nb * 512:(nb + 1) * 512], in0=o[:, nb * 512:(nb + 1) * 512],
                                                                      in1=xr[:, nb * 512:(nb + 1) * 512], op=ALU.add),
                 r=[("o", sl, nb), ("xr", sl)], w=[("o", sl, nb)])
        S.op("sp", lambda h, c=c, o=o: h.dma_start(out=out_d[c * 128:(c + 1) * 128, :], in_=o),
             r=[("o", sl, 0), ("o", sl, 1)], w=[("out", c)], dma="b_o%d" % sl)
    S.wait_all("sp", [("d:" + k, v) for k, v in S.dma_cnt.items() if k.startswith("b_o")])


B_NAMES = ["norm_pre", "norm_post", "b_w_in", "b_w_out"]


def build_b(stage=99):
    nc = bass.Bass("TRN2", target_bir_lowering=False)
    T = declare_inputs(nc, B_NAMES)
    T["rel_bias"] = nc.dram_tensor("rel_bias", [32, 48], F32, kind="ExternalInput")
    T["oh"] = nc.dram_tensor("oh", [3, 33, 512], BF16, kind="ExternalInput")
    T["vmask"] = nc.dram_tensor("vmask", [128, 69], F32, kind="ExternalInput")
    x1e = nc.dram_tensor("x1e", [4096, 1024], F32, kind="ExternalInput").ap()
    out_d = nc.dram_tensor("out", [2048, 1024], F32, kind="ExternalOutput").ap()
    S = Sched(nc)
    C = Ctx()
    setup_common(nc, S, C)
    phase_b(nc, S, C, x1e, out_d, T, stage)
    S.barrier()
    S.emit()
    return nc


def t5_bucket_np(rel):
    half = 16
    ret = np.where(rel > 0, half, 0)
    n = np.abs(rel)
    nf = np.maximum(n, 1).astype(np.float32)
    large = 8 + (np.log(nf / 8) / np.float32(np.log(1024 / 8)) * (half - 8)).astype(np.int32)
    large = np.minimum(large, half - 1)
    return ret + np.where(n < 8, n, large)


def onehot_const():
    import ml_dtypes
    oh = np.zeros((3, 33, 512), ml_dtypes.bfloat16)
    for g, (d, _) in enumerate(GROUPS):
        for ab in range(2):
            for m in range(256):
                delta = 127 - m
                if m == 255:
                    ok = False
                elif ab == 0:
                    ok = 0 <= delta <= 127
                    rel = delta - 64
                else:
                    ok = -127 <= delta <= 0
                    rel = delta + 64
                b = int(t5_bucket_np(np.array(rel * d))) if ok else 32
                oh[g, b, ab * 256 + m] = 1.0
    return oh


def vmask_const(lo, hi):
    vm = np.zeros((128, 69), np.float32)
    jj = np.arange(128)
    for g, (d, ntr) in enumerate(GROUPS):
        base = 1024 // d - 64
        for r in range(d):
            for j in range(ntr):
                e = (base + 128 * j + jj) * d + r
                vm[:, VT_OFF[g] + r * ntr + j] = ((e >= lo) & (e < hi)).astype(np.float32)
    return vm


def _core_geom(core):
    b, s0 = core // 4, (core % 4) * 2048
    return b, s0, s0 - 1024


def build_fused():
    from contextlib import ExitStack
    nc = bass.Bass("TRN2", target_bir_lowering=False)
    T = declare_inputs(nc, A_NAMES + ["b_w_in", "b_w_out"])
    T["rel_bias"] = nc.dram_tensor("rel_bias", [32, 48], F32, kind="ExternalInput")
    T["oh"] = nc.dram_tensor("oh", [3, 33, 512], BF16, kind="ExternalInput")
    T["vmask"] = nc.dram_tensor("vmask", [128, 69], F32, kind="ExternalInput")
    x_in = nc.dram_tensor("x_in", [4096, 1024], F32, kind="ExternalInput").ap()
    x1e = nc.dram_tensor("x1e_scr", [4096, 1024], F32).ap()
    out_d = nc.dram_tensor("out", [2048, 1024], F32, kind="ExternalOutput").ap()
    C = Ctx()
    C.alloc = nc.alloc_sbuf_tensor
    SA = Sched(nc, tag="A")
    setup_common(nc, SA, C)
    with ExitStack() as stk:
        C.alloc = lambda n, sh, dt: stk.enter_context(nc.sbuf_tensor(n, sh, dt))
        phase_a(nc, SA, C, x_in, x1e, 32, T)
        SA.barrier()
        SA.emit()
    C.alloc = nc.alloc_sbuf_tensor
    SB = Sched(nc, tag="B")
    phase_b(nc, SB, C, x1e, out_d, T)
    SB.emit()
    return nc


def kernel(**inputs):
    c = host_consts(inputs)
    x = np.ascontiguousarray(np.asarray(inputs["x"], dtype=np.float32))
    n = 8
    nc = build_fused()
    oh = onehot_const()
    rb = np.ascontiguousarray(np.asarray(inputs["rel_bias"], dtype=np.float32))
    maps = []
    for core in range(n):
        b, s0, lo = _core_geom(core)
        m = {k: c[k] for k in A_NAMES + ["b_w_in", "b_w_out"]}
        m["rel_bias"] = rb
        m["oh"] = oh
        a, z = max(lo, 0), min(lo + 4096, 8192)
        xe = np.zeros((4096, 1024), np.float32)
        xe[a - lo:z - lo] = x[b, a:z]
        m["x_in"] = xe
        m["vmask"] = vmask_const(a - lo, z - lo)
        maps.append(m)
    res = run_bass_kernel_spmd(nc, maps, core_ids=list(range(n)))
    out = np.zeros_like(x)
    for core in range(n):
        b, s0, _ = _core_geom(core)
        out[b, s0:s0 + 2048] = res.results[core]["out"]
    return out
```

```python
import numpy as np
import concourse.bass as bass
import concourse.mybir as mybir
from concourse.bass_utils import run_bass_kernel_spmd

F32 = mybir.dt.float32
BF16 = mybir.dt.bfloat16
AF = mybir.ActivationFunctionType
ALU = mybir.AluOpType
AX = mybir.AxisListType

ENGS = ("pe", "act", "dve", "pool", "sp")


class Sched:
    def __init__(self, nc, same_engine_sync=True, tag=""):
        self.nc = nc
        self.tag = tag
        self.ops = {e: [] for e in ENGS}
        self.last_write = {}
        self.readers = {}
        self.clock = {e: {} for e in ENGS}
        self.ev_clock = {}
        self.dma_cnt = {}
        self.same = same_engine_sync
        self.targets = set()

    def _need(self, eng, ev, waits):
        name, idx = ev
        if not self.same and name == eng:
            return
        ck = self.clock[eng]
        if ck.get(name, 0) >= idx:
            return
        waits.append(ev)
        self.targets.add(ev)
        for k, v in self.ev_clock.get(ev, {}).items():
            if ck.get(k, 0) < v:
                ck[k] = v
        ck[name] = idx

    def op(self, eng, fn, r=(), w=(), dma=None):
        waits = []
        for key in r:
            ev = self.last_write.get(key)
            if ev is not None and not (ev[0] == eng and eng == "pe"):
                self._need(eng, ev, waits)
        for key in w:
            ev = self.last_write.get(key)
            if ev is not None and not (ev[0] == eng and eng == "pe"):
                self._need(eng, ev, waits)
            for name, idx in self.readers.get(key, {}).items():
                if not (name == eng and eng == "pe"):
                    self._need(eng, (name, idx), waits)
        idx = len(self.ops[eng]) + 1
        if dma is not None:
            c = self.dma_cnt.get(dma, 0) + 1
            self.dma_cnt[dma] = c
            ev = ("d:" + dma, c)
        else:
            ev = (eng, idx)
        snap = dict(self.clock[eng])
        self.ev_clock[ev] = snap
        for key in r:
            self.readers.setdefault(key, {})
            cur = self.readers[key].get(ev[0], 0)
            if ev[1] > cur:
                self.readers[key][ev[0]] = ev[1]
        for key in w:
            self.last_write[key] = ev
            self.readers[key] = {}
        self.ops[eng].append(dict(fn=fn, waits=waits, ev=ev, dma=dma))
        return ev

    def barrier(self):
        evs = []
        for e in ENGS:
            n = 0
            for i, o in enumerate(self.ops[e]):
                if o["ev"] is not None and o["ev"][0] == e:
                    n = i + 1
            if n:
                evs.append((e, n))
        for k, v in self.dma_cnt.items():
            evs.append(("d:" + k, v))
        for e in ENGS:
            self.wait_all(e, [ev for ev in evs if not (ev[0] == e)])

    def wait_all(self, eng, evs):
        waits = []
        for ev in evs:
            self._need(eng, ev, waits)
        self.ops[eng].append(dict(fn=None, waits=waits, ev=None, dma=None))

    def emit(self):
        nc = self.nc
        esem = {e: nc.alloc_semaphore(name=self.tag + "s_" + e) for e in ENGS}
        dsem = {n: nc.alloc_semaphore(name=self.tag + "d_" + n) for n in self.dma_cnt}
        tval = {}
        for e in ENGS:
            c = 0
            for i, o in enumerate(self.ops[e]):
                ev = o["ev"]
                if ev is not None and ev[0] == e and ev in self.targets:
                    c += 1
                    tval[ev] = c
            assert c < 60000, (e, c)

        def run(e, h):
            for o in self.ops[e]:
                for (name, idx) in o["waits"]:
                    if name.startswith("d:"):
                        h.wait_ge(dsem[name[2:]], 16 * idx)
                    else:
                        h.wait_ge(esem[name], tval[(name, idx)])
                if o["fn"] is None:
                    continue
                ins = o["fn"](h)
                if o["dma"] is not None:
                    ins.then_inc(dsem[o["dma"]], 16)
                elif o["ev"] in self.targets:
                    ins.then_inc(esem[e], 1)

        with nc.Block() as block:
            @block.tensor
            def _(h):
                run("pe", h)

            @block.scalar
            def _(h):
                run("act", h)

            @block.vector
            def _(h):
                run("dve", h)

            @block.gpsimd
            def _(h):
                run("pool", h)

            @block.sync
            def _(h):
                run("sp", h)


D = 1024
AW = 2048
EPS = 1e-6


def bc_ap(t, off, n):
    return bass.AP(t, off, [[0, 128], [1, n]])


class Ctx:
    pass


def setup_common(nc, S, C):
    if not hasattr(C, "alloc"):
        C.alloc = nc.alloc_sbuf_tensor
    C.identf = nc.alloc_sbuf_tensor("identf", [128, 128], F32)
    C.ident = nc.alloc_sbuf_tensor("ident", [128, 128], BF16)
    identf, ident = C.identf, C.ident
    S.op("pool", lambda h: h.memset(identf[:], 0.0), w=["identf"])
    S.op("pool", lambda h: h.affine_select(out=identf[:], in_=identf[:], pattern=[[-1, 128]],
                                           compare_op=ALU.not_equal, fill=1.0, base=0,
                                           channel_multiplier=1), r=["identf"], w=["identf"])
    S.op("dve", lambda h: h.tensor_copy(out=ident[:], in_=identf[:]), r=["identf"], w=["ident"])
    C.ps = [nc.alloc_psum_tensor("ps%d" % i, [128, 512], F32) for i in range(8)]


def rstd_ops(S, ssq_ap, out_ap, n, rkeys, wkey):
    S.op("dve", lambda h: h.tensor_scalar(out=out_ap, in0=ssq_ap, scalar1=1.0 / n, scalar2=EPS,
                                          op0=ALU.mult, op1=ALU.add), r=rkeys, w=[wkey])
    S.op("act", lambda h: h.activation(out=out_ap, in_=out_ap, func=AF.Sqrt), r=[wkey], w=[wkey])
    S.op("dve", lambda h: h.reciprocal(out=out_ap, in_=out_ap), r=[wkey], w=[wkey])


def phase_a(nc, S, C, x_in, x_out, nch, T):
    ps = C.ps
    ident = C.ident
    w_in = C.alloc("a_win", [128, 8, 6144], BF16)
    w_out = C.alloc("a_wout", [128, 16, 1024], BF16)
    w_sT = C.alloc("a_wsT", [128, 16, 128], BF16)
    bsT = C.alloc("a_bsT", [128, 16], F32)
    gpre = C.alloc("a_gpre", [128, 1024], F32)
    gpost = C.alloc("a_gpost", [128, 1024], F32)
    vng = C.alloc("a_vng", [128, 2048], F32)
    vnb = C.alloc("a_vnb", [128, 2048], F32)
    win_d = T["a_w_in"].ap().rearrange("(k p) n -> p k n", p=128)
    wout_d = T["a_w_out"].ap().rearrange("(k p) n -> p k n", p=128)
    nb_order = [4, 5, 6, 7, 0, 1, 2, 3, 8, 9, 10, 11]
    S.op("sp", lambda h: h.dma_start(out=gpre[:], in_=bc_ap(T["norm_pre"], 0, 1024)), w=["gpre"], dma="c_gpre")
    for nb in nb_order:
        S.op("pool", lambda h, nb=nb: h.dma_start(out=w_in[:, :, nb * 512:(nb + 1) * 512],
                                                  in_=win_d[:, :, nb * 512:(nb + 1) * 512]),
             w=[("win", nb)], dma="c_win%d" % nb)
        if nb == 7:
            S.op("pool", lambda h: h.dma_start(out=w_sT[:], in_=T["a_w_sT"].ap()), w=["wsT"], dma="c_wsT")
    for nb in range(2):
        S.op("pool", lambda h, nb=nb: h.dma_start(out=w_out[:, :, nb * 512:(nb + 1) * 512],
                                                  in_=wout_d[:, :, nb * 512:(nb + 1) * 512]),
             w=[("wout", nb)], dma="c_wout%d" % nb)
    S.op("sp", lambda h: h.dma_start(out=vng[:], in_=bc_ap(T["a_vnorm_g"], 0, 2048)), w=["vng"], dma="c_vng")
    S.op("sp", lambda h: h.dma_start(out=vnb[:], in_=bc_ap(T["a_vnorm_b"], 0, 2048)), w=["vnb"], dma="c_vnb")
    S.op("sp", lambda h: h.dma_start(out=bsT[:], in_=T["a_b_sT"].ap()), w=["bsT"], dma="c_bsT")
    S.op("sp", lambda h: h.dma_start(out=gpost[:], in_=bc_ap(T["norm_post"], 0, 1024)), w=["gpost"], dma="c_gpost")

    NX = 2
    xs = [C.alloc("a_xs%d" % i, [128, 1024], F32) for i in range(NX)]
    hb = [C.alloc("a_hb%d" % i, [128, 1024], BF16) for i in range(2)]
    hT = [C.alloc("a_hT%d" % i, [128, 8, 128], BF16) for i in range(2)]
    st = [C.alloc("a_st%d" % i, [128, 8], F32) for i in range(NX)]
    junk = C.alloc("a_junk", [128, 1024], BF16)
    u = C.alloc("a_u", [128, 2048], BF16)
    sg = C.alloc("a_sg", [128, 2048], BF16)
    gv = C.alloc("a_gv", [128, 2048], F32)
    ot = C.alloc("a_ot", [128, 2048], F32)
    v = C.alloc("a_v", [128, 2048], BF16)
    us = u
    y = u
    yT = C.alloc("a_yT", [128, 16, 128], BF16)
    bst = C.alloc("a_bst", [128, 4, 6], F32)
    mv = C.alloc("a_mv", [128, 4], F32)
    zrot = [0]

    def zbank():
        b = zrot[0] % 4
        zrot[0] += 1
        return b

    zz = C.alloc("a_zz", [128, 2], F32)
    S.op("dve", lambda h: h.memset(zz[:], 0.0), w=["zz0", "zz1"])

    def pretrigger_gelu():
        S.op("act", lambda h: h.activation(out=zz[:, 1:2], in_=zz[:, 0:1], func=AF.Gelu_apprx_tanh), r=["zz0"], w=["zz1"])

    def LOAD(i):
        s = i % NX
        S.op("sp", lambda h: h.dma_start(out=xs[s][:], in_=x_in[i * 128:(i + 1) * 128, :]),
             w=[("xs", s)], dma="a_x%d" % s)
        S.op("act", lambda h: h.activation(out=junk[:], in_=xs[s][:], func=AF.Square, accum_out=st[s][:, 0:1]),
             r=[("xs", s)], w=["junk", ("st0", s)])
        rstd_ops(S, st[s][:, 0:1], st[s][:, 1:2], 1024, [("st0", s)], ("st1", s))
        pretrigger_gelu()
        S.op("dve", lambda h: h.scalar_tensor_tensor(out=hb[i % 2][:], in0=xs[s][:], scalar=st[s][:, 1:2],
                                                     in1=gpre[:], op0=ALU.mult, op1=ALU.mult),
             r=[("xs", s), ("st1", s), "gpre"], w=[("hb", i % 2)])

    def TH(i):
        b = zbank() if i > 0 else 6
        pb = ps[b][:].bitcast(BF16)
        for k in range(8):
            S.op("pe", lambda h, k=k: h.transpose(pb[:, k * 128:(k + 1) * 128], hb[i % 2][:, k * 128:(k + 1) * 128], ident[:]),
                 r=[("hb", i % 2), "ident"], w=[("ps", b)])
        S.op("act", lambda h: h.activation(out=hT[i % 2][:].rearrange("p k t -> p (k t)"), in_=pb[:, 0:1024], func=AF.Copy),
             r=[("ps", b)], w=[("hT", i % 2)])

    def Z(i):
        hTi = hT[i % 2]
        for nb in nb_order:
            b = zbank()
            for k in range(8):
                S.op("pe", lambda h, k=k, nb=nb, b=b: h.matmul(ps[b][:, :], lhsT=hTi[:, k, :],
                                                               rhs=w_in[:, k, nb * 512:(nb + 1) * 512],
                                                               start=(k == 0), stop=(k == 7)),
                     r=[("hT", i % 2), ("win", nb)], w=[("ps", b)])
            if nb < 4:
                S.op("act", lambda h, nb=nb, b=b: h.activation(out=u[:, nb * 512:(nb + 1) * 512], in_=ps[b][:, :],
                                                               func=AF.Gelu_apprx_tanh),
                     r=[("ps", b)], w=[("u", nb)])
            elif nb < 8:
                j = nb - 4
                S.op("act", lambda h, j=j, b=b: h.activation(out=gv[:, j * 512:(j + 1) * 512], in_=ps[b][:, :],
                                                             func=AF.Gelu_apprx_tanh),
                     r=[("ps", b)], w=[("gv", j)])
                S.op("dve", lambda h, j=j: h.bn_stats(out=bst[:, j, :], in_=gv[:, j * 512:(j + 1) * 512]),
                     r=[("gv", j)], w=[("bst", j)])
                if j == 3:
                    LN(i)
                    if i + 1 < nch:
                        LOAD(i + 1)
            else:
                j = nb - 8
                S.op("act", lambda h, j=j, b=b: h.activation(out=sg[:, j * 512:(j + 1) * 512], in_=ps[b][:, :],
                                                             func=AF.Silu),
                     r=[("ps", b)], w=[("sg", j)])

    def LN(i):
        S.op("dve", lambda h: h.bn_aggr(out=mv[:, 0:2], in_=bst[:].rearrange("p a b -> p (a b)")),
             r=[("bst", j) for j in range(4)], w=["mv"])
        S.op("dve", lambda h: h.tensor_scalar(out=mv[:, 2:3], in0=mv[:, 1:2], scalar1=1.0, scalar2=EPS, op0=ALU.mult, op1=ALU.add),
             r=["mv"], w=["mv2"])
        S.op("act", lambda h: h.activation(out=mv[:, 2:3], in_=mv[:, 2:3], func=AF.Sqrt), r=["mv2"], w=["mv2"])
        S.op("dve", lambda h: h.reciprocal(out=mv[:, 2:3], in_=mv[:, 2:3]), r=["mv2"], w=["mv2"])
        S.op("dve", lambda h: h.tensor_scalar(out=gv[:], in0=gv[:], scalar1=mv[:, 0:1], scalar2=mv[:, 2:3],
                                              op0=ALU.subtract, op1=ALU.mult),
             r=["mv", "mv2"] + [("gv", j) for j in range(4)], w=[("gv", j) for j in range(4)])
        S.op("pool", lambda h: h.tensor_tensor(out=gv[:], in0=gv[:], in1=vng[:], op=ALU.mult),
             r=["vng"] + [("gv", j) for j in range(4)], w=[("gv", j) for j in range(4)])
        S.op("pool", lambda h: h.tensor_tensor(out=v[:], in0=gv[:], in1=vnb[:], op=ALU.add),
             r=["vnb"] + [("gv", j) for j in range(4)], w=["v"])

    def SP(i):
        for hh in range(16):
            b = 4 + hh // 4
            S.op("pe", lambda h, hh=hh, b=b: h.matmul(ps[b][:, (hh % 4) * 128:(hh % 4 + 1) * 128], lhsT=w_sT[:, hh, :],
                                                      rhs=v[:, hh * 128:(hh + 1) * 128], start=True, stop=True),
                 r=["wsT", "v"], w=[("ps", b)])
            if hh % 4 == 3:
                for h2 in range(hh - 3, hh + 1):
                    if h2 % 4 == 0:
                        jq = h2 // 4
                        S.op("pool", lambda h, jq=jq: h.tensor_tensor(out=us[:, jq * 512:(jq + 1) * 512],
                                                                      in0=u[:, jq * 512:(jq + 1) * 512],
                                                                      in1=sg[:, jq * 512:(jq + 1) * 512], op=ALU.mult),
                             r=[("u", jq), ("sg", jq)], w=[("us", jq), ("u", jq)])
                    S.op("dve", lambda h, h2=h2, b=b: h.scalar_tensor_tensor(
                        out=y[:, h2 * 128:(h2 + 1) * 128], in0=ps[b][:, (h2 % 4) * 128:(h2 % 4 + 1) * 128],
                        scalar=bsT[:, h2:h2 + 1], in1=us[:, h2 * 128:(h2 + 1) * 128], op0=ALU.add, op1=ALU.mult),
                         r=[("ps", b), "bsT", ("us", h2 // 4)], w=[("y", h2 // 8), ("u", h2 // 4), ("us", h2 // 4)])

    def TY(i):
        for half in range(2):
            b = 6 + half
            pb = ps[b][:].bitcast(BF16)
            for k in range(8):
                c = half * 8 + k
                S.op("pe", lambda h, k=k, c=c, pb=pb: h.transpose(pb[:, k * 128:(k + 1) * 128], y[:, c * 128:(c + 1) * 128], ident[:]),
                     r=[("y", half), "ident"] + [("u", half * 2), ("u", half * 2 + 1)], w=[("ps", b)])
            S.op("act", lambda h, half=half, pb=pb: h.activation(out=yT[:, half * 8:(half + 1) * 8, :].rearrange("p k t -> p (k t)"),
                                                          in_=pb[:, 0:1024], func=AF.Copy),
                 r=[("ps", b)], w=[("yT", half)])

    obanks = {}

    def O_mm(i):
        s = i % NX
        bs = []
        for nb in range(2):
            b = zbank()
            bs.append(b)
            for k in range(16):
                S.op("pe", lambda h, k=k, nb=nb, b=b: h.matmul(ps[b][:, :], lhsT=yT[:, k, :],
                                                               rhs=w_out[:, k, nb * 512:(nb + 1) * 512],
                                                               start=(k == 0), stop=(k == 15)),
                     r=[("yT", k // 8), ("wout", nb)], w=[("ps", b)])
            S.op("act", lambda h, nb=nb, b=b: h.activation(out=junk[:, 0:512], in_=ps[b][:, :], func=AF.Square,
                                                           accum_out=st[s][:, 2 + nb:3 + nb]),
                 r=[("ps", b)], w=["junk", ("st2", s, nb)])
        obanks[i] = bs

    def O_post(i):
        s = i % NX
        bs = obanks[i]
        S.op("dve", lambda h: h.tensor_tensor(out=st[s][:, 4:5], in0=st[s][:, 2:3], in1=st[s][:, 3:4], op=ALU.add),
             r=[("st2", s, 0), ("st2", s, 1)], w=[("st4", s)])
        rstd_ops(S, st[s][:, 4:5], st[s][:, 5:6], 1024, [("st4", s)], ("st5", s))
        pretrigger_gelu()
        S.op("sp", lambda h: h.dma_start(out=ot[:, 1024:2048], in_=x_in[i * 128:(i + 1) * 128, :]),
             w=[("ot", 2), ("ot", 3)], dma="a_xr")
        for nb in range(2):
            b = bs[nb]
            S.op("dve", lambda h, nb=nb, b=b: h.scalar_tensor_tensor(
                out=ot[:, nb * 512:(nb + 1) * 512], in0=ps[b][:, :], scalar=st[s][:, 5:6],
                in1=gpost[:, nb * 512:(nb + 1) * 512], op0=ALU.mult, op1=ALU.mult),
                 r=[("ps", b), ("st5", s), "gpost"], w=[("ot", nb)])
            S.op("pool", lambda h, nb=nb: h.tensor_tensor(out=ot[:, nb * 512:(nb + 1) * 512], in0=ot[:, nb * 512:(nb + 1) * 512],
                                                          in1=ot[:, 1024 + nb * 512:1024 + (nb + 1) * 512], op=ALU.add),
                 r=[("ot", nb), ("ot", 2 + nb)], w=[("ot", nb)])
        S.op("sp", lambda h: h.dma_start(out=x_out[i * 128:(i + 1) * 128, :], in_=ot[:, 0:1024]),
             r=[("ot", 0), ("ot", 1)], w=[("x1", i)], dma="a_o")

    LOAD(0)
    TH(0)
    for i in range(nch):
        Z(i)
        SP(i)
        if i + 1 < nch:
            TH(i + 1)
        if i >= 1:
            O_mm(i - 1)
            O_post(i - 1)
        TY(i)
    O_mm(nch - 1)
    O_post(nch - 1)


def declare_inputs(nc, names):
    shapes = {
        "norm_pre": [2 * 1024], "norm_post": [2 * 1024],
        "a_w_in": [1024, 6144], "a_w_sT": [128, 16 * 128], "a_b_sT": [128, 16],
        "a_vnorm_g": [2048], "a_vnorm_b": [2048], "a_w_out": [2048, 1024],
        "b_w_in": [1024, 10240], "b_w_out": [1024, 1024],
    }
    return {n: nc.dram_tensor(n, shapes[n], F32, kind="ExternalInput") for n in names}


A_NAMES = ["norm_pre", "norm_post", "a_w_in", "a_w_sT", "a_b_sT", "a_vnorm_g", "a_vnorm_b", "a_w_out"]


def build_a(nch):
    nc = bass.Bass("TRN2", target_bir_lowering=False)
    T = declare_inputs(nc, A_NAMES)
    x_in = nc.dram_tensor("x_in", [nch * 128, 1024], F32, kind="ExternalInput").ap()
    x_out = nc.dram_tensor("x1", [nch * 128, 1024], F32, kind="ExternalOutput").ap()
    S = Sched(nc)
    C = Ctx()
    setup_common(nc, S, C)
    phase_a(nc, S, C, x_in, x_out, nch, T)
    S.wait_all("sp", [("d:" + k, v) for k, v in S.dma_cnt.items() if k.startswith("a_o")])
    S.emit()
    return nc


def host_consts(inp):
    f = lambda a: np.ascontiguousarray(np.asarray(a, dtype=np.float32))
    c = {
        "norm_pre": f(inp["norm_pre"]).reshape(-1), "norm_post": f(inp["norm_post"]).reshape(-1),
        "a_w_in": f(inp["a_w_in"])[0], "a_w_out": f(inp["a_w_out"])[0],
        "a_w_sT": f(np.transpose(f(inp["a_w_s"])[0], (2, 0, 1))).reshape(128, 16 * 128),
        "a_b_sT": f(f(inp["a_b_s"])[0].T),
        "a_vnorm_g": f(inp["a_vnorm_g"])[0], "a_vnorm_b": f(inp["a_vnorm_b"])[0],
        "b_w_in": f(inp["b_w_in"])[0], "b_w_out": f(inp["b_w_out"])[0],
    }
    return c


GROUPS = ((1, 17), (4, 5), (16, 2))
VT_OFF = (0, 17, 37)


def phase_b(nc, S, C, x1e, out_d, T, stage=99):
    ps = C.ps
    ident = C.ident
    identf = C.identf
    gpre = C.alloc("b_gpre", [128, 1024], F32)
    gpost = C.alloc("b_gpost", [128, 1024], F32)
    tab = C.alloc("b_tab", [33, 48], F32)
    Jf = C.alloc("b_Jf", [128, 128], F32)
    J = C.alloc("b_J", [128, 128], BF16)
    vm = C.alloc("b_vm", [128, 69], F32)
    rbcf = [C.alloc("b_rbc%d" % i, [128, 2048], F32) for i in range(2)]
    rbc = [t[0:64] for t in rbcf]
    rdram = nc.dram_tensor("b_rdram", [4 * 2048], F32)
    r16 = C.alloc("b_r16", [128, 2, 16], F32)

    tvec = nc.dram_tensor("b_tvec", [3 * 16 * 512], BF16)
    hT = C.alloc("b_hT", [128, 8, 4096], BF16)
    yTa = C.alloc("b_yT", [128, 8, 2048], BF16)
    scr = C.alloc("b_scr", [128, 4096], F32)
    vtall = C.alloc("b_vtall", [128, 69 * 130], BF16)
    vt = [vtall[:, VT_OFF[g] * 130:(VT_OFF[g] + GROUPS[g][0] * GROUPS[g][1]) * 130].rearrange("p (t h c) -> p t h c", h=2, c=65)
          for g in range(3)]
    wout = vtall[:, 0:8192].rearrange("p (k n) -> p k n", k=8)
    QT = C.alloc("b_QT", [128, 2048], BF16)
    KT = C.alloc("b_KT", [128, 4096], BF16)
    gsil = C.alloc("b_gsil", [128, 2, 2048], BF16)[0:64]
    ohs = KT[0:33, 0:1536].rearrange("p (g n) -> p g n", g=3)
    tvs = QT[0:16, 0:1536].rearrange("p (g n) -> p g n", g=3)
    wqkv = [C.alloc("b_wqkv%d" % i, [128, 3, 8, 128], BF16) for i in range(2)]
    wg = [C.alloc("b_wg%d" % i, [128, 8, 128], BF16) for i in range(2)]
    es = [C.alloc("b_es%d" % i, [128, 512], BF16) for i in range(2)]
    pt = [C.alloc("b_pt%d" % i, [128, 512], BF16) for i in range(4)]
    h2 = [C.alloc("b_h2%d" % i, [128, 512], BF16) for i in range(2)]
    eb = [C.alloc("b_eb%d" % i, [128, 512], BF16) for i in range(2)]
    st = C.alloc("b_st", [128, 4, 8], F32)
    win_d = T["b_w_in"].ap().rearrange("(k p) n -> p k n", p=128)

    S.op("sp", lambda h: h.dma_start(out=gpre[:], in_=bc_ap(T["norm_pre"], 1024, 1024)), w=["gpre"], dma="b_c0")
    S.op("dve", lambda h: h.memset(st[:, 0, 6:8], 0.0), w=["stz", "stx"])
    S.op("sp", lambda h: h.dma_start(out=gpost[:], in_=bc_ap(T["norm_post"], 1024, 1024)), w=["gpost"], dma="b_c1")
    S.op("sp", lambda h: h.dma_start(out=vm[:], in_=T["vmask"].ap()), w=["vm"], dma="b_c2")
    S.op("sp", lambda h: h.dma_start(out=ohs, in_=T["oh"].ap().rearrange("g b n -> b g n")), w=["ohs"], dma="b_c3")
    S.op("dve", lambda h: h.memset(tab[:], -30000.0), w=["tab"])
    S.op("sp", lambda h: h.dma_start(out=tab[0:32, :], in_=T["rel_bias"].ap()), w=["tab"], dma="b_c4")
    S.op("pool", lambda h: h.memset(Jf[:], 0.0), w=["Jf"])
    S.op("pool", lambda h: h.affine_select(out=Jf[:], in_=Jf[:], pattern=[[1, 128]], compare_op=ALU.not_equal,
                                           fill=1.0, base=-127, channel_multiplier=1), r=["Jf"], w=["Jf"])
    S.op("dve", lambda h: h.tensor_copy(out=J[:], in_=Jf[:]), r=["Jf"], w=["J"])
    tabh = C.alloc("b_tabh", [33, 48], BF16)
    tabl = C.alloc("b_tabl", [33, 48], BF16)
    S.op("dve", lambda h: h.tensor_copy(out=tabh[:], in_=tab[:]), r=["tab"], w=["tabh"])
    S.op("dve", lambda h: h.tensor_tensor(out=tabl[:], in0=tab[:], in1=tabh[:], op=ALU.subtract), r=["tab", "tabh"], w=["tabl"])
    for g in range(3):
        S.op("pe", lambda h, g=g: h.matmul(ps[0][0:16, :], lhsT=tabh[0:33, g * 16:(g + 1) * 16], rhs=ohs[0:33, g, :],
                                           start=True, stop=False), r=["tabh", "ohs"], w=[("ps", 0)])
        S.op("pe", lambda h, g=g: h.matmul(ps[0][0:16, :], lhsT=tabl[0:33, g * 16:(g + 1) * 16], rhs=ohs[0:33, g, :],
                                           start=False, stop=True), r=["tabl", "ohs"], w=[("ps", 0)])
        S.op("act", lambda h, g=g: h.activation(out=tvs[:, g, :], in_=ps[0][0:16, :], func=AF.Exp),
             r=[("ps", 0)], w=[("tvs", g)])
        S.op("sp", lambda h, g=g: h.dma_start(out=bass.AP(tvec, g * 16 * 512, [[512, 16], [1, 512]]), in_=tvs[:, g, :]),
             r=[("tvs", g)], w=[("tvec", g)], dma="b_tv%d" % g)
    for g in range(3):
        nt = GROUPS[g][0] * GROUPS[g][1]
        for hh in range(2):
            S.op("dve", lambda h, g=g, hh=hh, nt=nt: h.tensor_copy(out=vt[g][:, :, hh, 64], in_=vm[:, VT_OFF[g]:VT_OFF[g] + nt]),
                 r=["vm"], w=[("vtv", g)])

    NB1 = 4
    xs = [scr[:, i * 1024:(i + 1) * 1024] for i in range(NB1)]
    hbv = rbcf[0][:, :].bitcast(BF16)
    hb = [hbv[:, i * 1024:(i + 1) * 1024] for i in range(NB1)]
    junk = rbcf[1][:, 0:512].bitcast(BF16)

    def LOADB(i):
        s = i % NB1
        S.op("sp", lambda h: h.dma_start(out=xs[s], in_=x1e[i * 128:(i + 1) * 128, :]), w=[("xs", s)], dma="b_x%d" % s)
        S.op("act", lambda h: h.activation(out=junk, in_=xs[s], func=AF.Square, accum_out=st[:, s, 0:1]),
             r=[("xs", s)], w=["junk", ("st0", s)])
        rstd_ops(S, st[:, s, 0:1], st[:, s, 1:2], 1024, [("st0", s)], ("st1", s))
        S.op("dve", lambda h: h.scalar_tensor_tensor(out=hb[s], in0=xs[s], scalar=st[:, s, 1:2], in1=gpre[:],
                                                     op0=ALU.mult, op1=ALU.mult),
             r=[("xs", s), ("st1", s), "gpre"], w=[("hb", s)])

    def THB(i):
        s = i % NB1
        b = 4 + s
        pb = ps[b][:].bitcast(BF16)
        for k in range(8):
            S.op("pe", lambda h, k=k: h.transpose(pb[:, k * 128:(k + 1) * 128], hb[s][:, k * 128:(k + 1) * 128], ident[:]),
                 r=[("hb", s), "ident"], w=[("ps", b)])
        S.op("act" if i % 2 else "dve",
             (lambda h: h.activation(out=hT[:, :, i * 128:(i + 1) * 128], in_=pb[:, 0:1024].rearrange("p (k t) -> p k t", k=8), func=AF.Copy))
             if i % 2 else
             (lambda h: h.tensor_copy(out=hT[:, :, i * 128:(i + 1) * 128], in_=pb[:, 0:1024].rearrange("p (k t) -> p k t", k=8))),
             r=[("ps", b)], w=[("hT", i // 4)])

    if stage < 1:
        return
    for i in range(NB1 - 1):
        LOADB(i)
    for i in range(32):
        if i + NB1 - 1 < 32:
            LOADB(i + NB1 - 1)
        THB(i)
    S.barrier()
    if stage < 2:
        return

    acc = [scr[:, 0:2048], scr[:, 2048:4096]]
    rot = {"p": 0, "s": 0, "o": 0, "e": 0}

    def pbank():
        rot["p"] += 1
        return rot["p"] % 2

    def load_w(hp, g):
        sl = (hp * 3 + g) % 2
        for w3 in range(3):
            c0 = g * 3072 + w3 * 1024 + hp * 128
            S.op("pool", lambda h, w3=w3, c0=c0: h.dma_start(out=wqkv[sl][:, w3, :, :], in_=win_d[:, :, c0:c0 + 128]),
                 w=[("wqkv", sl, w3)], dma="b_w%d_%d" % (sl, w3))
        if g == 0:
            c0 = 9216 + hp * 128
            S.op("pool", lambda h, c0=c0: h.dma_start(out=wg[hp % 2][:], in_=win_d[:, :, c0:c0 + 128]),
                 w=[("wg", hp % 2)], dma="b_wg%d" % (hp % 2))
        e = (hp * 3 + g) % 2
        off = ((g * 16 + hp * 2) * 2) * 256
        S.op("sp", lambda h: h.dma_start(out=h2[e][:].rearrange("p (a i) -> p a i", a=4),
                                         in_=bass.AP(tvec, off, [[1, 128], [256, 4], [1, 128]])),
             r=[("tvec", g)], w=[("h2", e)], dma="b_h2%d" % e)

    def make_eb(hp, g):
        e = (hp * 3 + g) % 2
        b = pbank()
        S.op("pe", lambda h: h.matmul(ps[b][:, :], lhsT=J[:], rhs=h2[e][:], start=True, stop=True),
             r=["J", ("h2", e)], w=[("ps", b)])
        S.op("dve", lambda h: h.tensor_copy(out=eb[e][:], in_=ps[b][:, :]), r=[("ps", b)], w=[("eb", e)])

    def proj(hp, g, parts="qkv"):
        sl = (hp * 3 + g) % 2
        d, ntr = GROUPS[g]
        if "q" in parts:
            for blk in range(4):
                b = pbank()
                for k in range(8):
                    S.op("pe", lambda h, k=k, blk=blk, b=b: h.matmul(ps[b][:, :], lhsT=wqkv[sl][:, 0, k, :],
                                                                     rhs=hT[:, k, 1024 + blk * 512:1024 + (blk + 1) * 512],
                                                                     start=(k == 0), stop=(k == 7)),
                         r=[("wqkv", sl, 0)], w=[("ps", b)])
                S.op("act", lambda h, blk=blk, b=b: h.activation(out=QT[:, blk * 512:(blk + 1) * 512], in_=ps[b][:, :],
                                                                 func=AF.Copy, scale=0.125),
                     r=[("ps", b)], w=[("QT", blk)])
        if "k" in parts:
            kstart, nkb = ((896, 5), (768, 5), (0, 8))[g]
            for j in range(nkb):
                c0 = kstart + j * 512
                b = pbank()
                for k in range(8):
                    S.op("pe", lambda h, k=k, c0=c0, b=b: h.matmul(ps[b][:, :], lhsT=wqkv[sl][:, 1, k, :],
                                                                   rhs=hT[:, k, c0:c0 + 512],
                                                                   start=(k == 0), stop=(k == 7)),
                         r=[("wqkv", sl, 1)], w=[("ps", b)])
                S.op("act", lambda h, c0=c0, b=b: h.activation(out=KT[:, c0:c0 + 512], in_=ps[b][:, :], func=AF.Copy),
                     r=[("ps", b)], w=[("KT", j)])
        if "v" in parts:
            for u in v_units(hp, g):
                u()

    def v_units(hp, g):
        sl = (hp * 3 + g) % 2
        d, ntr = GROUPS[g]
        base = 1024 // d - 64
        nt = d * ntr
        units = []
        TPU = 2
        for t0 in range(0, nt, TPU):
            def unit(t0=t0):
                b = pbank()
                n = min(TPU, nt - t0)
                for tt in range(n):
                    idx = t0 + tt
                    r, j = idx // ntr, idx % ntr
                    e0 = (base + 128 * j) * d + r
                    for k in range(8):
                        S.op("pe", lambda h, k=k, tt=tt, e0=e0, b=b: h.matmul(
                            ps[b][:, tt * 128:(tt + 1) * 128],
                            lhsT=hT[:, k, e0:e0 + 127 * d + 1:d], rhs=wqkv[sl][:, 2, k, :],
                            start=(k == 0), stop=(k == 7)),
                             r=[("wqkv", sl, 2)], w=[("ps", b)])
                S.op("act", lambda h, t0=t0, n=n, b=b: h.activation(
                    out=vt[g][:, t0:t0 + n, :, 0:64],
                    in_=ps[b][:, 0:n * 128].rearrange("p (t h c) -> p t h c", t=n, h=2), func=AF.Copy),
                     r=[("ps", b)], w=[("vt", g)])
            units.append(unit)
        return units

    def gate(hp):
        for blk in range(4):
            b = pbank()
            for k in range(8):
                S.op("pe", lambda h, k=k, blk=blk, b=b: h.matmul(ps[b][:, :], lhsT=wg[hp % 2][:, k, :],
                                                                 rhs=hT[:, k, 1024 + blk * 512:1024 + (blk + 1) * 512],
                                                                 start=(k == 0), stop=(k == 7)),
                     r=[("wg", hp % 2)], w=[("ps", b)])
            for hh in range(2):
                S.op("act", lambda h, blk=blk, b=b, hh=hh: h.activation(
                    out=gsil[:, hh, blk * 512:(blk + 1) * 512], in_=ps[b][hh * 64:(hh + 1) * 64, :], func=AF.Silu),
                     r=[("ps", b)], w=[("gsil", blk)])

    def attn(hp, g, fillers=()):
        fillers = list(fillers)
        d, ntr = GROUPS[g]
        e = (hp * 3 + g) % 2
        base = 1024 // d - 64
        qs = 1024 // d
        if g == 0:
            banks = [[(0, 4 * tb + q) for q in range(4)] for tb in range(4)]
        elif g == 1:
            banks = [[(r, q) for q in range(4)] for r in range(4)]
        else:
            banks = [[(4 * rb + q, 0) for q in range(4)] for rb in range(4)]
        units = [(bi, qi, r, t) for bi, bk in enumerate(banks) for qi, (r, t) in enumerate(bk)]

        def ST(p):
            bk = (2, 3) if rot["s"] % 2 == 0 else (4, 5)
            rot["s"] += 1
            for qq in range(2):
                bi, qi, r, t = units[2 * p + qq]
                q0 = (qs + 128 * t) * d + r - 1024
                for ab in range(2):
                    k0 = (base + 128 * (t + ab)) * d + r
                    col = (qq * 2 + ab) * 128
                    for hh in range(2):
                        S.op("pe", lambda h, hh=hh, k0=k0, col=col, q0=q0, bk=bk: h.matmul(
                            ps[bk[hh]][:, col:col + 128],
                            lhsT=KT[hh * 64:(hh + 1) * 64, k0:k0 + 127 * d + 1:d],
                            rhs=QT[hh * 64:(hh + 1) * 64, q0:q0 + 127 * d + 1:d], start=True, stop=True),
                             r=[("KT", kb) for kb in range(8)] + [("QT", qb) for qb in range(4)], w=[("ps", bk[hh])])
            return bk

        def EP(p, bk):
            xs_ = []
            for hh in range(2):
                x = hh * 2 + (rot["e"] % 2)
                xs_.append(x)
                ebh = bass.AP(eb[e], hh * 256, [[512, 128], [0, 2], [1, 256]])
                S.op("act", lambda h, hh=hh, bk=bk: h.activation(out=es[hh][:], in_=ps[bk[hh]][:, :], func=AF.Exp),
                     r=[("ps", bk[hh])], w=[("es", hh)])
                S.op("dve", lambda h, hh=hh, x=x, ebh=ebh: h.tensor_tensor(
                    out=pt[x][:].rearrange("p (q c) -> p q c", q=2), in0=es[hh][:].rearrange("p (q c) -> p q c", q=2),
                    in1=ebh, op=ALU.mult),
                     r=[("es", hh), ("eb", e)], w=[("pt", x)])
            rot["e"] += 1
            return xs_

        def PV(p, xs_):
            for hh in range(2):
                for qq in range(2):
                    bi, qi, r, t = units[2 * p + qq]
                    ob = 6
                    for ab in range(2):
                        idx = r * ntr + t + ab
                        c0 = (qq * 2 + ab) * 128
                        S.op("pe", lambda h, hh=hh, ab=ab, idx=idx, ob=ob, qi=qi, c0=c0, x=xs_[hh]: h.matmul(
                            ps[ob + hh][0:65, qi * 128:(qi + 1) * 128],
                            lhsT=vt[g][:, idx, hh, :], rhs=pt[x][:, c0:c0 + 128],
                            start=(ab == 0), stop=(ab == 1)),
                             r=[("vt", g), ("vtv", g), ("pt", xs_[hh])], w=[("ps", ob + hh)])
            for qq in range(2):
                bi, qi, r, t = units[2 * p + qq]
                ob = 6
                if qi == 3:
                    for hh in range(2):
                        src = ps[ob + hh][0:65, :]
                        if g == 0:
                            S.op("act", lambda h, hh=hh, src=src, bi=bi: h.activation(out=acc[hh][0:65, bi * 512:(bi + 1) * 512], in_=src, func=AF.Copy),
                                 r=[("ps", ob + hh)], w=[("acc", hh)])
                        elif g == 1:
                            dst = acc[hh][0:65, r:r + 511 * 4 + 1:4]
                            S.op("dve", lambda h, dst=dst, src=src: h.tensor_tensor(out=dst, in0=dst, in1=src, op=ALU.add),
                                 r=[("ps", ob + hh), ("acc", hh)], w=[("acc", hh)])
                        else:
                            r0 = r - 3
                            dst = acc[hh][0:65, :].rearrange("p (i r) -> p r i", r=16)[:, r0:r0 + 4, :]
                            S.op("dve", lambda h, dst=dst, src=src: h.tensor_tensor(
                                out=dst, in0=dst, in1=src.rearrange("p (r i) -> p r i", r=4), op=ALU.add),
                                 r=[("ps", ob + hh), ("acc", hh)], w=[("acc", hh)])

        npairs = len(units) // 2
        bks = {0: ST(0)}
        for p in range(npairs):
            if p + 1 < npairs:
                bks[p + 1] = ST(p + 1)
            xs_ = EP(p, bks[p])
            nf = -(-len(fillers) // (npairs - p))
            for _ in range(nf):
                fillers.pop(0)()
            PV(p, xs_)
        for f in fillers:
            f()

    def fin1a(hp):
        for hh in range(2):
            S.op("sp", lambda h, hh=hh: h.dma_start(out=bass.AP(rdram, hh * 2048, [[2048, 1], [1, 2048]]), in_=acc[hh][64:65, :]),
                 r=[("acc", hh)], w=[("rdram", hh)], dma="b_rd%d" % hh)
            S.op("sp", lambda h, hh=hh: h.dma_start(out=r16[:, hh, :], in_=bass.AP(rdram, hh * 2048, [[16, 128], [1, 16]])),
                 r=[("rdram", hh)], w=[("r16", hh)], dma="b_r16%d" % hh)

    def fin1b(hp):
        for hh in range(2):
            S.op("dve", lambda h, hh=hh: h.reciprocal(out=r16[:, hh, :], in_=r16[:, hh, :]), r=[("r16", hh)], w=[("r16", hh)])
            S.op("sp", lambda h, hh=hh: h.dma_start(out=bass.AP(rdram, 4096 + hh * 2048, [[16, 128], [1, 16]]), in_=r16[:, hh, :]),
                 r=[("r16", hh)], w=[("rdram2", hh)], dma="b_rd2%d" % hh)
            S.op("sp", lambda h, hh=hh: h.dma_start(out=rbc[hh][:, :], in_=bass.AP(rdram, 4096 + hh * 2048, [[0, 64], [1, 2048]])),
                 r=[("rdram2", hh)], w=[("rbc", hh)], dma="b_rb%d" % hh)

    def fin_pre(hp):
        for hh in range(2):
            S.op("dve", lambda h, hh=hh: h.tensor_tensor(out=acc[hh][0:64, :], in0=acc[hh][0:64, :], in1=gsil[:, hh, :], op=ALU.mult),
                 r=[("acc", hh)] + [("gsil", blk) for blk in range(4)], w=[("acc", hh)])

    def fin2(hp):
        for hh in range(2):
            S.op("dve", lambda h, hh=hh: h.tensor_tensor(out=yTa[hh * 64:(hh + 1) * 64, hp, :], in0=acc[hh][0:64, :],
                                                         in1=rbc[hh][:, :], op=ALU.mult),
                 r=[("acc", hh), ("rbc", hh)], w=[("yTa", hp)])

    load_w(0, 0)
    seq = [(hp, g) for hp in range(8) for g in range(3)]
    for n_, (hp, g) in enumerate(seq):
        nxt = seq[n_ + 1] if n_ + 1 < len(seq) else None
        if nxt is not None:
            load_w(*nxt)
        make_eb(hp, g)
        first = (n_ == 0)
        if g == 0 and hp > 0:
            fin_pre(hp - 1)
            proj(hp, g, "q")
            fin1b(hp - 1)
            gate(hp)
            fin2(hp - 1)
            S.op("act", lambda h: h.activation(out=st[:, 0, 6:7], in_=st[:, 0, 7:8], func=AF.Exp), r=["stz"], w=["stx"])
            proj(hp, g, "k")
        else:
            proj(hp, g, "qkv" if first else "qk")
            if g == 0:
                gate(hp)
        attn(hp, g, v_units(*nxt) if nxt is not None else ())
        if g == 2:
            fin1a(hp)
    S.op("pool", lambda h: h.dma_start(out=wout, in_=T["b_w_out"].ap().rearrange("(k p) n -> p k n", p=128)),
         w=["wout"] + [("vt", g_) for g_ in range(3)] + [("vtv", g_) for g_ in range(3)], dma="b_wout")
    fin_pre(7)
    fin1b(7)
    fin2(7)
    S.barrier()

    slots = [(scr[:, 0:1024], scr[:, 1024:2048]), (scr[:, 2048:3072], scr[:, 3072:4096]),
             (rbcf[0][:, 0:1024], rbcf[0][:, 1024:2048]), (rbcf[1][:, 0:1024], rbcf[1][:, 1024:2048])]
    for c in range(16):
        sl = c % 4
        o, xr = slots[sl]
        S.op("act", lambda h, c=c, xr=xr: h.dma_start(out=xr, in_=x1e[1024 + c * 128:1024 + (c + 1) * 128, :]),
             w=[("xr", sl)], dma="b_xr%d" % sl)
        bs = []
        for nb in range(2):
            b = 2 * sl + nb
            bs.append(b)
            for k in range(8):
                S.op("pe", lambda h, k=k, nb=nb, b=b, c=c: h.matmul(ps[b][:, :], lhsT=yTa[:, k, c * 128:(c + 1) * 128],
                                                                    rhs=wout[:, k, nb * 512:(nb + 1) * 512],
                                                                    start=(k == 0), stop=(k == 7)),
                     r=["wout"] + [("yTa", k) for k in range(8)], w=[("ps", b)])
            S.op("act", lambda h, nb=nb, b=b, sl=sl: h.activation(out=pt[sl][:], in_=ps[b][:, :], func=AF.Square,
                                                                 accum_out=st[:, sl, 2 + nb:3 + nb]),
                 r=[("ps", b)], w=[("pt", sl), ("st2", sl, nb)])
        S.op("dve", lambda h, sl=sl: h.tensor_tensor(out=st[:, sl, 4:5], in0=st[:, sl, 2:3], in1=st[:, sl, 3:4], op=ALU.add),
             r=[("st2", sl, 0), ("st2", sl, 1)], w=[("st4", sl)])
        rstd_ops(S, st[:, sl, 4:5], st[:, sl, 5:6], 1024, [("st4", sl)], ("st5", sl))
        for nb in range(2):
            b = bs[nb]
            S.op("dve", lambda h, nb=nb, b=b, sl=sl, o=o: h.scalar_tensor_tensor(
                out=o[:, nb * 512:(nb + 1) * 512], in0=ps[b][:, :], scalar=st[:, sl, 5:6],
                in1=gpost[:, nb * 512:(nb + 1) * 512], op0=ALU.mult, op1=ALU.mult),
                 r=[("ps", b), ("st5", sl), "gpost"], w=[("o", sl, nb)])
            S.op("pool", lambda h, nb=nb, o=o, xr=xr: h.tensor_tensor(out=o[:, nb * 512:(nb + 1) * 512], in0=o[:, nb * 512:(nb + 1) * 512],
                                                                      in1=xr[:, nb * 512:(nb + 1) * 512], op=ALU.add),
                 r=[("o", sl, nb), ("xr", sl)], w=[("o", sl, nb)])
        S.op("sp", lambda h, c=c, o=o: h.dma_start(out=out_d[c * 128:(c + 1) * 128, :], in_=o),
             r=[("o", sl, 0), ("o", sl, 1)], w=[("out", c)], dma="b_o%d" % sl)
    S.wait_all("sp", [("d:" + k, v) for k, v in S.dma_cnt.items() if k.startswith("b_o")])


B_NAMES = ["norm_pre", "norm_post", "b_w_in", "b_w_out"]


def build_b(stage=99):
    nc = bass.Bass("TRN2", target_bir_lowering=False)
    T = declare_inputs(nc, B_NAMES)
    T["rel_bias"] = nc.dram_tensor("rel_bias", [32, 48], F32, kind="ExternalInput")
    T["oh"] = nc.dram_tensor("oh", [3, 33, 512], BF16, kind="ExternalInput")
    T["vmask"] = nc.dram_tensor("vmask", [128, 69], F32, kind="ExternalInput")
    x1e = nc.dram_tensor("x1e", [4096, 1024], F32, kind="ExternalInput").ap()
    out_d = nc.dram_tensor("out", [2048, 1024], F32, kind="ExternalOutput").ap()
    S = Sched(nc)
    C = Ctx()
    setup_common(nc, S, C)
    phase_b(nc, S, C, x1e, out_d, T, stage)
    S.barrier()
    S.emit()
    return nc


def t5_bucket_np(rel):
    half = 16
    ret = np.where(rel > 0, half, 0)
    n = np.abs(rel)
    nf = np.maximum(n, 1).astype(np.float32)
    large = 8 + (np.log(nf / 8) / np.float32(np.log(1024 / 8)) * (half - 8)).astype(np.int32)
    large = np.minimum(large, half - 1)
    return ret + np.where(n < 8, n, large)


def onehot_const():
    import ml_dtypes
    oh = np.zeros((3, 33, 512), ml_dtypes.bfloat16)
    for g, (d, _) in enumerate(GROUPS):
        for ab in range(2):
            for m in range(256):
                delta = 127 - m
                if m == 255:
                    ok = False
                elif ab == 0:
                    ok = 0 <= delta <= 127
                    rel = delta - 64
                else:
                    ok = -127 <= delta <= 0
                    rel = delta + 64
                b = int(t5_bucket_np(np.array(rel * d))) if ok else 32
                oh[g, b, ab * 256 + m] = 1.0
    return oh


def vmask_const(lo, hi):
    vm = np.zeros((128, 69), np.float32)
    jj = np.arange(128)
    for g, (d, ntr) in enumerate(GROUPS):
        base = 1024 // d - 64
        for r in range(d):
            for j in range(ntr):
                e = (base + 128 * j + jj) * d + r
                vm[:, VT_OFF[g] + r * ntr + j] = ((e >= lo) & (e < hi)).astype(np.float32)
    return vm


def _core_geom(core):
    b, s0 = core // 4, (core % 4) * 2048
    return b, s0, s0 - 1024


def build_fused():
    from contextlib import ExitStack
    nc = bass.Bass("TRN2", target_bir_lowering=False)
    T = declare_inputs(nc, A_NAMES + ["b_w_in", "b_w_out"])
    T["rel_bias"] = nc.dram_tensor("rel_bias", [32, 48], F32, kind="ExternalInput")
    T["oh"] = nc.dram_tensor("oh", [3, 33, 512], BF16, kind="ExternalInput")
    T["vmask"] = nc.dram_tensor("vmask", [128, 69], F32, kind="ExternalInput")
    x_in = nc.dram_tensor("x_in", [4096, 1024], F32, kind="ExternalInput").ap()
    x1e = nc.dram_tensor("x1e_scr", [4096, 1024], F32).ap()
    out_d = nc.dram_tensor("out", [2048, 1024], F32, kind="ExternalOutput").ap()
    C = Ctx()
    C.alloc = nc.alloc_sbuf_tensor
    SA = Sched(nc, tag="A")
    setup_common(nc, SA, C)
    with ExitStack() as stk:
        C.alloc = lambda n, sh, dt: stk.enter_context(nc.sbuf_tensor(n, sh, dt))
        phase_a(nc, SA, C, x_in, x1e, 32, T)
        SA.barrier()
        SA.emit()
    C.alloc = nc.alloc_sbuf_tensor
    SB = Sched(nc, tag="B")
    phase_b(nc, SB, C, x1e, out_d, T)
    SB.emit()
    return nc


def kernel(**inputs):
    c = host_consts(inputs)
    x = np.ascontiguousarray(np.asarray(inputs["x"], dtype=np.float32))
    n = 8
    nc = build_fused()
    oh = onehot_const()
    rb = np.ascontiguousarray(np.asarray(inputs["rel_bias"], dtype=np.float32))
    maps = []
    for core in range(n):
        b, s0, lo = _core_geom(core)
        m = {k: c[k] for k in A_NAMES + ["b_w_in", "b_w_out"]}
        m["rel_bias"] = rb
        m["oh"] = oh
        a, z = max(lo, 0), min(lo + 4096, 8192)
        xe = np.zeros((4096, 1024), np.float32)
        xe[a - lo:z - lo] = x[b, a:z]
        m["x_in"] = xe
        m["vmask"] = vmask_const(a - lo, z - lo)
        maps.append(m)
    res = run_bass_kernel_spmd(nc, maps, core_ids=list(range(n)))
    out = np.zeros_like(x)
    for core in range(n):
        b, s0, _ = _core_geom(core)
        out[b, s0:s0 + 2048] = res.results[core]["out"]
    return out
```
